# Optimizing a Trainium2 kernel written in Bass

```python
import math
import jax, jax.numpy as jnp
from jax import lax
import numpy as np

D_MODEL = 2048
BATCH = 4
SEQ = 4096
DEPTH = 1

MLA_NOPE = 128
MLA_ROPE = 64
MLA_V = 128
MLA_HEADS = D_MODEL // MLA_V
MLA_Q_RANK = 768
MLA_KV_RANK = 512
MLA_QK = MLA_NOPE + MLA_ROPE
ROPE_THETA = 10000.0
SWA_HEAD_DIM = 64
SWA_HEADS = D_MODEL // SWA_HEAD_DIM
SWA_KV_HEADS = 4
SWA_GROUP = SWA_HEADS // SWA_KV_HEADS
WINDOW = 128
BLOCK = 128
REL_BUCKETS = 32
REL_MAX_DIST = 128
D_FF = 5632
CONV_WIDTH = 3
EPS = 1e-6
NEG = -1e30

MLA_IN = MLA_Q_RANK + MLA_KV_RANK + MLA_ROPE
SWA_Q = SWA_HEADS * SWA_HEAD_DIM
SWA_KV = SWA_KV_HEADS * SWA_HEAD_DIM
N_BRANCH = 2
IN_COLS = MLA_IN + SWA_Q + 2 * SWA_KV + N_BRANCH * D_MODEL

kernel_name = "hybrid_mla_swa_convffn_block"


def rms_norm(x, g):
    xf = x.astype(jnp.float32)
    y = xf * lax.rsqrt(jnp.mean(xf * xf, axis=-1, keepdims=True) + EPS)
    return (y * g.astype(jnp.float32)).astype(x.dtype)


def rope_tables(seq):
    pos = jnp.arange(seq, dtype=jnp.float32)
    inv = ROPE_THETA ** (-jnp.arange(0, MLA_ROPE, 2, dtype=jnp.float32) / MLA_ROPE)
    ang = pos[:, None] * inv[None, :]
    ang = jnp.concatenate([ang, ang], axis=-1)
    return jnp.cos(ang), jnp.sin(ang)


def apply_rope(x, cos, sin):
    half = x.shape[-1] // 2
    x1, x2 = x[..., :half], x[..., half:]
    rot = jnp.concatenate([-x2, x1], axis=-1)
    return x * cos.astype(x.dtype) + rot * sin.astype(x.dtype)


def t5_bucket(dist):
    max_exact = REL_BUCKETS // 2
    n = jnp.maximum(dist, 0)
    large = max_exact + (jnp.log(jnp.maximum(n, 1).astype(jnp.float32) / max_exact)
                         / math.log(REL_MAX_DIST / max_exact)
                         * (REL_BUCKETS - max_exact)).astype(jnp.int32)
    large = jnp.minimum(large, REL_BUCKETS - 1)
    return jnp.where(n < max_exact, n, large)


def mla_branch(cq, ckv, k_rope, g_q, w_uq, g_kv, w_ukv):
    B, S, _ = cq.shape
    nb = S // BLOCK
    cos, sin = rope_tables(S)
    q = (rms_norm(cq, g_q) @ w_uq).reshape(B, S, MLA_HEADS, MLA_QK)
    q_nope = q[..., :MLA_NOPE]
    q_rope = apply_rope(q[..., MLA_NOPE:], cos[:, None, :], sin[:, None, :])
    kv = (rms_norm(ckv, g_kv) @ w_ukv).reshape(B, S, MLA_HEADS, MLA_NOPE + MLA_V)
    k_nope, v = kv[..., :MLA_NOPE], kv[..., MLA_NOPE:]
    k_rope = apply_rope(k_rope, cos, sin)
    scale = MLA_QK ** -0.5
    qn_blocks = q_nope.reshape(B, nb, BLOCK, MLA_HEADS, MLA_NOPE).transpose(1, 0, 2, 3, 4)
    qr_blocks = q_rope.reshape(B, nb, BLOCK, MLA_HEADS, MLA_ROPE).transpose(1, 0, 2, 3, 4)
    kpos = jnp.arange(S)

    def one_block(args):
        qn, qr, i = args
        s = (jnp.einsum('bqhd,bkhd->bhqk', qn, k_nope)
             + jnp.einsum('bqhd,bkd->bhqk', qr, k_rope)).astype(jnp.float32) * scale
        qpos = i * BLOCK + jnp.arange(BLOCK)
        s = jnp.where(kpos[None, :] <= qpos[:, None], s, NEG)
        p = jax.nn.softmax(s, axis=-1).astype(v.dtype)
        return jnp.einsum('bhqk,bkhd->bqhd', p, v)

    out = lax.map(one_block, (qn_blocks, qr_blocks, jnp.arange(nb)))
    return out.transpose(1, 0, 2, 3, 4).reshape(B, S, MLA_HEADS * MLA_V)


def swa_branch(q, k, v, rel_bias, sinks):
    B, S, _ = q.shape
    nb = S // BLOCK
    q = q.reshape(B, nb, BLOCK, SWA_KV_HEADS, SWA_GROUP, SWA_HEAD_DIM)

    def band(t):
        t = t.reshape(B, S, SWA_KV_HEADS, SWA_HEAD_DIM)
        t = jnp.pad(t, ((0, 0), (BLOCK, 0), (0, 0), (0, 0)))
        t = t.reshape(B, nb + 1, BLOCK, SWA_KV_HEADS, SWA_HEAD_DIM)
        return jnp.concatenate([t[:, :-1], t[:, 1:]], axis=2)

    kb, vb = band(k), band(v)
    s = jnp.einsum('bnqhgd,bnshd->bhgnqs', q, kb).astype(jnp.float32) * (SWA_HEAD_DIM ** -0.5)
    a = jnp.arange(BLOCK)
    bidx = jnp.arange(2 * BLOCK)
    dist = BLOCK + a[:, None] - bidx[None, :]
    bias = rel_bias[t5_bucket(dist)].astype(jnp.float32)
    bias = bias.transpose(2, 0, 1).reshape(SWA_KV_HEADS, SWA_GROUP, 1, BLOCK, 2 * BLOCK)
    kpos = jnp.arange(nb)[:, None, None] * BLOCK - BLOCK + bidx[None, None, :]
    mask = (dist >= 0)[None] & (dist < WINDOW)[None] & (kpos >= 0)
    s = jnp.where(mask, s + bias, NEG)
    sink = sinks.astype(jnp.float32).reshape(SWA_KV_HEADS, SWA_GROUP, 1, 1, 1)
    m = jnp.maximum(jnp.max(s, axis=-1, keepdims=True), sink)
    e = jnp.exp(s - m)
    p = e / (jnp.sum(e, axis=-1, keepdims=True) + jnp.exp(sink - m))
    o = jnp.einsum('bhgnqs,bnshd->bnqhgd', p.astype(vb.dtype), vb)
    return o.reshape(B, S, SWA_Q)


def causal_dwconv(u, w, b):
    S = u.shape[1]
    up = jnp.pad(u, ((0, 0), (CONV_WIDTH - 1, 0), (0, 0)))
    y = b
    for j in range(CONV_WIDTH):
        y = y + w[j] * up[:, j:j + S]
    return y


def setup_inputs(seed: int = 0) -> dict:
    key = jax.random.key(seed)
    ks = jax.random.split(key, 24)
    f32 = jnp.float32
    D, L = D_MODEL, DEPTH

    def nrm(k, shape, scale):
        return jax.random.normal(k, shape, f32) * scale

    def gain(k, shape):
        return 1.0 + 0.1 * jax.random.normal(k, shape, f32)

    return {
        "x": nrm(ks[0], (BATCH, SEQ, D), 1.0),
        "c": nrm(ks[1], (BATCH, D), 1.0),
        "w_ada": nrm(ks[2], (L, D, 6 * D), 0.5 * D ** -0.5),
        "b_ada": nrm(ks[3], (L, 6 * D), 0.02),
        "g_pre_mix": gain(ks[4], (L, D)),
        "g_post_mix": gain(ks[5], (L, D)),
        "w_in": nrm(ks[6], (L, D, IN_COLS), D ** -0.5),
        "g_q_lat": gain(ks[7], (L, MLA_Q_RANK)),
        "w_uq": nrm(ks[8], (L, MLA_Q_RANK, MLA_HEADS * MLA_QK), MLA_Q_RANK ** -0.5),
        "g_kv_lat": gain(ks[9], (L, MLA_KV_RANK)),
        "w_ukv": nrm(ks[10], (L, MLA_KV_RANK, MLA_HEADS * (MLA_NOPE + MLA_V)), MLA_KV_RANK ** -0.5),
        "rel_bias": nrm(ks[11], (REL_BUCKETS, SWA_HEADS), 0.5),
        "sinks": nrm(ks[12], (L, SWA_HEADS), 1.0),
        "w_o": nrm(ks[13], (L, D, D), D ** -0.5),
        "g_pre_ffn": gain(ks[14], (L, D)),
        "g_post_ffn": gain(ks[15], (L, D)),
        "w_up": nrm(ks[16], (L, D, 2 * D_FF), D ** -0.5),
        "conv_w": nrm(ks[17], (L, CONV_WIDTH, 2 * D_FF), CONV_WIDTH ** -0.5),
        "conv_b": nrm(ks[18], (L, 2 * D_FF), 0.02),
        "w_down": nrm(ks[19], (L, D_FF, D), D_FF ** -0.5),
    }


def reference(x, c, w_ada, b_ada, g_pre_mix, g_post_mix, w_in, g_q_lat, w_uq, g_kv_lat,
              w_ukv, rel_bias, sinks, w_o, g_pre_ffn, g_post_ffn, w_up, conv_w, conv_b, w_down):
    D = D_MODEL
    c_act = jax.nn.silu(c)
    for l in range(DEPTH):
        mod = (c_act @ w_ada[l] + b_ada[l])[:, None, :]
        sh1, sc1, gt1, sh2, sc2, gt2 = jnp.split(mod, 6, axis=-1)

        h = rms_norm(x, g_pre_mix[l]) * (1.0 + sc1) + sh1
        z = h @ w_in[l]
        o0 = 0
        cq = z[..., o0:o0 + MLA_Q_RANK]; o0 += MLA_Q_RANK
        ckv = z[..., o0:o0 + MLA_KV_RANK]; o0 += MLA_KV_RANK
        kr = z[..., o0:o0 + MLA_ROPE]; o0 += MLA_ROPE
        qs = z[..., o0:o0 + SWA_Q]; o0 += SWA_Q
        ks_ = z[..., o0:o0 + SWA_KV]; o0 += SWA_KV
        vs = z[..., o0:o0 + SWA_KV]; o0 += SWA_KV
        gates = jax.nn.sigmoid(z[..., o0:o0 + N_BRANCH * D])
        g_a, g_b = gates[..., :D], gates[..., D:]

        o_a = mla_branch(cq, ckv, kr, g_q_lat[l], w_uq[l], g_kv_lat[l], w_ukv[l])
        o_b = swa_branch(qs, ks_, vs, rel_bias, sinks[l])
        mix = (g_a * o_a + g_b * o_b) @ w_o[l]
        x = x + gt1 * rms_norm(mix, g_post_mix[l])

        h = rms_norm(x, g_pre_ffn[l]) * (1.0 + sc2) + sh2
        u = causal_dwconv(h @ w_up[l], conv_w[l], conv_b[l])
        y = (jax.nn.silu(u[..., :D_FF]) * u[..., D_FF:]) @ w_down[l]
        x = x + gt2 * rms_norm(y, g_post_ffn[l])
    return x
```

```python
import math
import numpy as np
import concourse.bass as bass
import concourse.mybir as mybir

F32 = mybir.dt.float32
BF16 = mybir.dt.bfloat16
AF = mybir.ActivationFunctionType
ALU = mybir.AluOpType
ISZ = {F32: 4, BF16: 2}


class V:
    def __init__(self, ap, space, p0, p1, b0, b1):
        self.ap, self.space, self.p0, self.p1, self.b0, self.b1 = ap, space, p0, p1, b0, b1

    def reg(self):
        return (self.space, self.p0, self.p1, self.b0, self.b1)


class Arena:
    def __init__(self, ap, space, nbytes):
        self.ap = ap
        self.space = space
        self.nbytes = nbytes

    def view(self, boff, shape, dt, p0=0, np_=128):
        return Tn(self, boff, list(shape), dt, p0, np_)


class Tn:
    def __init__(self, arena, boff, shape, dt, p0, np_):
        self.arena, self.boff, self.shape, self.dt, self.p0, self.np_ = arena, boff, shape, dt, p0, np_
        isz = ISZ[dt]
        n = int(np.prod(shape))
        self.nbytes = n * isz
        assert boff % 4 == 0 and (n * isz) % 4 == 0, (boff, n, isz)
        assert boff + self.nbytes <= arena.nbytes, (boff, self.nbytes, arena.nbytes)
        a = arena.ap[p0:p0 + np_, boff // 4:(boff + n * isz) // 4]
        if dt != F32:
            a = a.bitcast(dt)
        if len(shape) == 2:
            a = a.rearrange("p (a b) -> p a b", a=shape[0])
        elif len(shape) == 3:
            a = a.rearrange("p (a b c) -> p a b c", a=shape[0], b=shape[1])
        elif len(shape) == 4:
            a = a.rearrange("p (a b c d) -> p a b c d", a=shape[0], b=shape[1], c=shape[2])
        self.full = a
        st = [isz]
        for d in reversed(shape[1:]):
            st.insert(0, st[0] * d)
        self.strides = st

    def __getitem__(self, idx):
        if not isinstance(idx, tuple):
            idx = (idx,)
        idx = list(idx) + [slice(None)] * (len(self.shape) - len(idx))
        lo = self.boff
        hi = self.boff
        apidx = [slice(None)]
        p0, p1 = self.p0, self.p0 + self.np_
        for d, (i, s, n) in enumerate(zip(idx, self.strides, self.shape)):
            if isinstance(i, int):
                assert 0 <= i < n, (i, n)
                lo += i * s
                hi += i * s
                apidx.append(i)
            else:
                a = 0 if i.start is None else i.start
                b = n if i.stop is None else i.stop
                assert 0 <= a < b <= n, (a, b, n, self.shape, idx)
                lo += a * s
                hi += (b - 1) * s
                apidx.append(slice(a, b))
        hi += ISZ[self.dt]
        return V(self.full[tuple(apidx)], self.arena.space, p0, p1, lo, hi)

    def parts(self, q0, q1):
        t = Tn.__new__(Tn)
        t.__dict__.update(self.__dict__)
        t.p0 = self.p0 + q0
        t.np_ = q1 - q0
        idx = [slice(q0, q1)] + [slice(None)] * len(self.shape)
        t.full = self.full[tuple(idx)]
        return t

    def all(self):
        return self[tuple([slice(None)] * len(self.shape))]


def DV(ap, name, lo=0, hi=1 << 40):
    return V(ap, "d:" + name, 0, 1, lo, hi)


class Prog:
    ENG = ["pe", "act", "dve", "pool", "sp"]

    def __init__(self, nc, n_dma_sems=8):
        self.nc = nc
        self.ops = {e: [] for e in self.ENG}
        self.recs = {}
        self.waited = {e: {} for e in self.ENG}
        self.n_dma_sems = n_dma_sems
        self.dma_rr = {e: 0 for e in self.ENG}
        self.dma_cnt = {}
        self.all_dma_events = []

    def _add_wait(self, eng, op, ev):
        if ev is None:
            return
        if ev[0] == "e":
            key = ("e", ev[1])
            val = ev[2]
            self.ops[ev[1]][ev[2]]["sig"] = True
        else:
            key = ("d", ev[1])
            val = ev[2]
        w = self.waited[eng]
        if w.get(key, -1) >= val:
            return
        w[key] = val
        op["waits"].append(ev)

    def _deps(self, eng, op, reads, writes, event):
        for v, is_w in [(r, False) for r in reads] + [(w, True) for w in writes]:
            if v.space == "ps":
                v = V(v.ap, "ps", 0, 128, v.b0 // 2048 * 2048, (v.b1 + 2047) // 2048 * 2048)
            lst = self.recs.setdefault(v.space, [])
            keep = []
            myidx = len(self.ops[eng])
            for r in lst:
                if r[4] == eng and r[5] == myidx:
                    keep.append(r)
                    continue
                overlap = not (r[1] <= v.p0 or v.p1 <= r[0] or r[3] <= v.b0 or v.b1 <= r[2])
                if overlap and (is_w or r[6] or (v.space == "ps" and r[4] != eng)):
                    if not (r[4] == "pe" and eng == "pe"):
                        self._add_wait(eng, op, r[7])
                if is_w and overlap and r[0] >= v.p0 and r[1] <= v.p1 and r[2] >= v.b0 and r[3] <= v.b1:
                    continue
                if (not is_w) and (not r[6]) and r[4] == eng and eng != "sp" and event[0] == "e" \
                        and r[0] == v.p0 and r[1] == v.p1 and r[2] == v.b0 and r[3] == v.b1:
                    continue
                keep.append(r)
            keep.append([v.p0, v.p1, v.b0, v.b1, eng, len(self.ops[eng]), is_w, event])
            self.recs[v.space] = keep

    def op(self, eng, fn, reads=(), writes=(), acc=False):
        o = {"fn": fn, "waits": [], "sig": False, "dma": None, "acc": acc}
        ev = ("e", eng, len(self.ops[eng]))
        self._deps(eng, o, reads, writes, ev)
        self.ops[eng].append(o)
        return ev

    def dma(self, eng, out, in_, extra_reads=(), extra_writes=()):
        k = self.dma_rr[eng] % self.n_dma_sems
        self.dma_rr[eng] += 1
        key = (eng, k)
        prev = self.dma_cnt.get(key, 0)
        o = {"waits": [], "sig": False, "dma": key, "acc": False}
        if prev > 0:
            self._add_wait(eng, o, ("d", key, prev))
        cnt = prev + 16
        self.dma_cnt[key] = cnt
        ev = ("d", key, cnt)
        oap, iap = out.ap, in_.ap
        o["fn"] = lambda e: e.dma_start(out=oap, in_=iap)
        self._deps(eng, o, [in_] + list(extra_reads), [out] + list(extra_writes), ev)
        self.ops[eng].append(o)
        self.all_dma_events.append(ev)
        return ev

    def wait_events(self, eng, events):
        o = {"fn": None, "waits": [], "sig": False, "dma": None, "acc": False}
        for ev in events:
            self._add_wait(eng, o, ev)
        self.ops[eng].append(o)

    def emit(self, block, sems):
        nc = self.nc
        cnts = {}
        for e in self.ENG:
            c = 0
            arr = []
            for o in self.ops[e]:
                if o["sig"]:
                    c += 1
                arr.append(c)
            cnts[e] = arr

        def run(e, engobj):
            for i, o in enumerate(self.ops[e]):
                for ev in o["waits"]:
                    if ev[0] == "e":
                        engobj.wait_ge(sems["e"][ev[1]], cnts[ev[1]][ev[2]])
                    else:
                        engobj.wait_ge(sems["d"][ev[1]], ev[2])
                if o["fn"] is None:
                    continue
                inst = o["fn"](engobj)
                if o["dma"] is not None:
                    inst.then_inc(sems["d"][o["dma"]], 16)
                elif o["sig"]:
                    inst.then_inc(sems["e"][e], 1)

        if self.ops["pe"]:
            @block.tensor
            def _(eng):
                run("pe", eng)
        if self.ops["act"]:
            @block.scalar
            def _(eng):
                run("act", eng)
        if self.ops["dve"]:
            @block.vector
            def _(eng):
                run("dve", eng)
        if self.ops["pool"]:
            @block.gpsimd
            def _(eng):
                run("pool", eng)
        if self.ops["sp"]:
            @block.sync
            def _(eng):
                run("sp", eng)

import contextlib
from concourse.bass_utils import run_bass_kernel_spmd

D = 2048
NKB = 32
NOWN = 17
NOTH = 15
OWN0 = NOTH * 128
WOWN = NOWN * 128
DFF = 5632
EPS = 1e-6
ARENA_BYTES = 206 * 1024
ST_BLOCKS = [(0, 9), (9, 17)]
SCALE_MLA = 192 ** -0.5
SCALE_SWA = 64 ** -0.5


class Bump:
    def __init__(self, arena, base, limit):
        self.a, self.o, self.base, self.limit = arena, base, base, limit

    def reset(self):
        self.o = self.base

    def get(self, shape, dt, p0=0, np_=128):
        n = int(np.prod(shape)) * ISZ[dt]
        n4 = (n + 3) // 4 * 4
        t = self.a.view(self.o, shape, dt, p0, np_)
        self.o += n4
        assert self.o <= self.limit, ("SBUF phase overflow", self.o, self.limit)
        return t


def build(dbg=False, stop=99):
    nc = bass.Bass("TRN2", target_bir_lowering=False)
    ext = lambda n, s, dt=F32: nc.dram_tensor(n, list(s), dt, kind="ExternalInput").ap()
    xk = ext("xk", [4096, D])
    c_in = ext("c", [1, D])
    w_ada = ext("w_ada", [D, 6 * D])
    b_ada = ext("b_ada", [1, 6 * D])
    grow = ext("grow", [4, D])
    w_in = ext("w_in", [D, 8000])
    w_uq = ext("w_uq", [768, 3072])
    w_ukv = ext("w_ukv", [512, 4096])
    w_o = ext("w_o", [D, D])
    w_up = ext("w_up", [D, 2 * DFF])
    w_down = ext("w_down", [DFF, D])
    cst = ext("cst", [128, 512])
    vecs = ext("vecs", [128, 10 + 88 * 4])
    cs_in = ext("cs", [2, 64, 4096])
    swab = ext("swab", [2, 128, 32 * 128])
    esink_in = ext("sinkrow", [1, 32 * 128])
    out = nc.dram_tensor("out", [2048, D], F32, kind="ExternalOutput").ap()
    okind = "ExternalOutput" if dbg else "Internal"
    itn = lambda n, s, dt: nc.dram_tensor(n, list(s), dt, kind=okind).ap()
    K_d = itn("K_d", [16, 128, 4096], BF16)
    V_d = itn("V_d", [32, 128, 2048], BF16)
    hT_d = itn("hT_d", [16, 128, WOWN], BF16)
    qn_d = itn("qn_d", [16, 128, WOWN], BF16)
    qr_d = itn("qr_d", [16, 64, WOWN], BF16)
    ga_d = itn("ga_d", [16, 128, WOWN], BF16)
    ks_d = itn("ks_d", [4, 128, 18 * 128], BF16)
    vs_d = itn("vs_d", [18, 128, 512], BF16)
    x1_d = itn("x1_d", [2048, D], F32)
    kr_d = itn("kr_d", [64, 4096], BF16)
    gt_d = itn("gt_d", [16, 128, WOWN], BF16)

    es = contextlib.ExitStack()
    es.enter_context(nc.allow_non_contiguous_dma(reason="layout"))
    ar = es.enter_context(nc.sbuf_tensor("arena", [128, ARENA_BYTES // 4], F32))
    pst = es.enter_context(nc.psum_tensor("ps", [128, 8 * 512], F32))
    A = Arena(ar, "sb", ARENA_BYTES)
    PSA = Arena(pst, "ps", 16 * 1024)
    P = Prog(nc)
    bank = [PSA.view(i * 2048, [512], F32) for i in range(8)]
    bankb = [PSA.view(i * 2048, [512], BF16) for i in range(8)]
    big = [PSA.view(i * 8192, [2048], F32) for i in range(2)]

    def mm(o, l, r, start=True, stop=True):
        P.op("pe", lambda e: e.matmul(o.ap, l.ap, r.ap, start=start, stop=stop), [l, r], [o])

    def tr(o, i, ident):
        P.op("pe", lambda e: e.transpose(o.ap, i.ap, ident.ap), [i, ident], [o])

    def act(o, i, func, scale=None, bias=None, accum=None, eng="act"):
        rd = [i]
        kw = {}
        if scale is not None:
            if isinstance(scale, V):
                rd.append(scale); kw["scale"] = scale.ap
            else:
                kw["scale"] = float(scale)
        if bias is not None:
            rd.append(bias); kw["bias"] = bias.ap
        wr = [o]
        if accum is not None:
            wr.append(accum); kw["accum_out"] = accum.ap
        P.op("act", lambda e: e.activation(out=o.ap, in_=i.ap, func=func, **kw), rd, wr)

    def tt(o, a, b, op, eng="dve"):
        P.op(eng, lambda e: e.tensor_tensor(out=o.ap, in0=a.ap, in1=b.ap, op=op), [a, b], [o])

    def ts(o, a, s1, s2, op0, op1=None, eng="dve"):
        rd = [a]
        v1 = s1.ap if isinstance(s1, V) else s1
        v2 = s2.ap if isinstance(s2, V) else s2
        if isinstance(s1, V): rd.append(s1)
        if isinstance(s2, V): rd.append(s2)
        if op1 is None:
            P.op(eng, lambda e: e.tensor_scalar(out=o.ap, in0=a.ap, scalar1=v1, scalar2=None, op0=op0), rd, [o])
        else:
            P.op(eng, lambda e: e.tensor_scalar(out=o.ap, in0=a.ap, scalar1=v1, scalar2=v2, op0=op0, op1=op1), rd, [o])

    def stt(o, a, s, b, op0, op1):
        rd = [a, b]
        sv = s.ap if isinstance(s, V) else s
        if isinstance(s, V): rd.append(s)
        P.op("dve", lambda e: e.scalar_tensor_tensor(out=o.ap, in0=a.ap, scalar=sv, in1=b.ap, op0=op0, op1=op1), rd, [o])

    def cp(o, i, eng="dve"):
        if eng == "act":
            act(o, i, AF.Copy)
        else:
            P.op(eng, lambda e: e.tensor_copy(out=o.ap, in_=i.ap), [i], [o])

    def recip(o, i):
        P.op("dve", lambda e: e.reciprocal(out=o.ap, in_=i.ap), [i], [o])

    def dma(o, i, q="sp"):
        return P.dma(q, o, i)

    def dreg(name, ap, lo=0, hi=1 << 40):
        return DV(ap, name, lo, hi)

    _rr = {"i": 0}

    def evac_eng():
        _rr["i"] += 1
        return "act" if _rr["i"] % 2 else "dve"

    pb = Bump(A, 0, ARENA_BYTES)
    ident_f = pb.get([128], F32)
    ident_b = pb.get([128], BF16)
    ones_b = pb.get([128], BF16)
    tri_b = pb.get([128], BF16)
    Rm_b = pb.get([64], BF16, 0, 64)
    cstf = pb.get([512], F32)
    epsc = cstf[448:449]
    bias_oth = cstf[449:450]
    flag = cstf[450:451]
    tinyc = cstf[451:452]
    vec = pb.get([10 + 88 * 4], F32)
    gq = lambda c: vec[c:c + 1]
    gkv = lambda c: vec[6 + c:7 + c]
    cw = lambda c, j: vec[10 + c * 3 + j:11 + c * 3 + j]
    cb = lambda c: vec[10 + 264 + c:11 + 264 + c]
    modv = pb.get([64], F32)
    r1_b = pb.get([2048], F32)
    r2_b = pb.get([2048], F32)
    krope_full = pb.get([4096], BF16)
    krope = krope_full.parts(0, 64)
    carry = pb.get([88, 2], F32)
    one_row = pb.get([128], F32, 0, 1)
    onepad = pb.get([128], BF16)
    one_rowb = onepad.parts(0, 1)
    esink_full = pb.get([32 * 128], BF16)
    esink_row = esink_full.parts(0, 1)
    cTb = pb.get([16], BF16)
    PBASE = pb.o
    ph = Bump(A, PBASE, ARENA_BYTES)

    P.op("pool", lambda e: e.memset(krope_full.parts(64, 128).all().ap, 0.0), [], [krope_full.parts(64, 128).all()])
    cstv = dreg("cst", cst)
    dma(cstf.all(), cstv)
    dma(ident_f.all(), dreg("cst", cst[:, 0:128]))
    dma(ident_b.all(), dreg("cst", cst[:, 0:128]), "pool")
    dma(ones_b.all(), dreg("cst", cst[:, 128:256]), "pool")
    dma(tri_b.all(), dreg("cst", cst[:, 256:384]), "pool")
    dma(Rm_b.all(), dreg("cst", cst[0:64, 384:448]), "pool")
    dma(vec.all(), dreg("vecs", vecs))
    dma(one_row.all(), dreg("cst", cst[0:1, 128:256]))
    P.op("pool", lambda e: e.memset(onepad.all().ap, 0.0), [], [onepad.all()])
    P.op("pool", lambda e: e.memset(esink_full.all().ap, 0.0), [], [esink_full.all()])
    dma(one_rowb.all(), dreg("cst", cst[0:1, 128:256]), "pool")
    sk_f = ph.get([32 * 128], F32, 0, 1)
    dma(sk_f.all(), dreg("sinkrow", esink_in))
    act(esink_row.all(), sk_f.all(), AF.Exp)

    ph.reset()
    cT = ph.get([16], F32)
    mod_row = ph.get([2 * D], F32, 0, 1)
    g_rows = ph.get([4, D], F32, 0, 1)
    pan = [ph.get([16, 1024], BF16) for _ in range(2)]
    dma(cT.all(), dreg("c", c_in.rearrange("o (kc p) -> p (o kc)", p=128)))
    dma(mod_row.all(), dreg("b_ada", b_ada[:, 0:2 * D]))
    dma(g_rows.all(), dreg("grow", grow.rearrange("(o g) d -> o g d", o=1)))
    act(cTb.all(), cT.all(), AF.Silu)
    wav = w_ada.rearrange("(kc p) n -> p kc n", p=128)
    for hv in range(4):
        pn = pan[hv % 2]
        dma(pn.all(), dreg("w_ada", wav[:, :, hv * 1024:(hv + 1) * 1024]), "pool")
        for nb in range(2):
            bk = bank[(hv * 2 + nb) % 4].parts(0, 1)
            for kc in range(16):
                mm(bk.all(), cTb[kc:kc + 1], pn[kc, nb * 512:(nb + 1) * 512], kc == 0, kc == 15)
            c0 = hv * 1024 + nb * 512
            tt(mod_row[c0:c0 + 512], bk.all(), mod_row[c0:c0 + 512], ALU.add)
    mrow = lambda v: mod_row[v * D:(v + 1) * D]
    stt(mrow(1), mrow(1), 1.0, g_rows[0], ALU.add, ALU.mult)
    srcs = [lambda f: mod_row[1 * D + f * 128:1 * D + (f + 1) * 128], lambda f: mod_row[0 * D + f * 128:0 * D + (f + 1) * 128]]
    for vi in range(2):
        for f in range(16):
            mm(bank[4][vi * 16 + f:vi * 16 + f + 1], srcs[vi](f), one_row[0:1])
    cp(modv[0:32], bank[4][0:32], "act")
    a1 = lambda f: modv[f:f + 1]
    sh1 = lambda f: modv[16 + f:17 + f]
    a2 = lambda f: modv[32 + f:33 + f]
    sh2 = lambda f: modv[48 + f:49 + f]
    if stop == 0:
        return finish(nc, es, P, locals())

    def rope(src_bank, w, cosv, sinv, dst, tmp):
        kb_, t1, t2, rb = tmp
        tt(t1, src_bank, cosv, ALU.mult)
        cp(kb_, src_bank, "act")
        mm(rb, Rm_b.all(), kb_)
        tt(t2, rb, sinv, ALU.mult)
        tt(dst, t1, t2, ALU.add, eng="pool")

    def lat_norm(nch, src_sb, sq, gfun, dst_fun, w, ssbank, rstd, inv_n):
        for c in range(nch):
            mm(ssbank, ones_b.all(), sq[c, 0:w], c == 0, c == nch - 1)
        act(rstd, ssbank, AF.Sqrt, scale=inv_n, bias=epsc)
        recip(rstd, rstd)
        if stop == 1.17:
            return
        for c in range(nch):
            stt(dst_fun(c), src_sb[c, 0:w], gfun(c), rstd, ALU.mult, ALU.mult)

    ph.reset()
    wkv = ph.get([16, 1088], BF16)
    wukv = ph.get([4, 4096], BF16)
    xt = [ph.get([2048], F32) for _ in range(2)]
    xs = ph.get([4, 2048], BF16)
    hT = ph.get([16, 512], BF16)
    ckv_sb = ph.get([4, 512], F32)
    sq = ph.get([4, 512], BF16)
    rstd_b = ph.get([512], F32)
    ckvn = ph.get([4, 512], BF16)
    kr_b = ph.get([512], BF16, 0, 64)
    rt1 = ph.get([512], F32, 0, 64)
    rt2 = ph.get([512], F32, 0, 64)
    cosb = ph.get([512], F32, 0, 64)
    sinb = ph.get([512], F32, 0, 64)
    kst = [ph.get([4, 512], BF16) for _ in range(2)]
    vst = [ph.get([2048], BF16) for _ in range(2)]
    ks_st = ph.get([2, 512], BF16)
    vs_st = [ph.get([4, 2, 64], BF16) for _ in range(2)]
    ssq = ph.get([8], F32)
    winv = w_in.rearrange("(kc p) n -> p kc n", p=128)
    dma(wkv[:, 0:576], dreg("w_in", winv[:, :, 768:1344]), "pool")
    dma(wkv[:, 576:1088], dreg("w_in", winv[:, :, 3392:3904]), "pool")
    wukvv = w_ukv.rearrange("(kc p) n -> p kc n", p=128)
    for hh in range(2):
        dma(wukv[:, hh * 2048:(hh + 1) * 2048], dreg("w_ukv", wukvv[:, :, hh * 2048:(hh + 1) * 2048]), "pool")
    wukv4 = A.view(wukv.boff, [4, 16, 256], BF16)
    pbk = {"i": 0}

    def nbank(lo=2, n=5):
        pbk["i"] += 1
        return lo + pbk["i"] % n

    def norm_transpose(xsrc_row0, nblk, dstT, w, afun, shfun, xs_, xt_, ssq_, x_dram, x_name):
        for blk in range(nblk):
            xb = xt_[blk % 2]
            dma(xb.all(), dreg(x_name, x_dram[xsrc_row0 + blk * 128: xsrc_row0 + (blk + 1) * 128, :]))
            act(xs_[blk], xb.all(), AF.Square, accum=ssq_[blk:blk + 1])
            act(ssq_[4 + blk:5 + blk], ssq_[blk:blk + 1], AF.Sqrt, scale=1.0 / D, bias=epsc)
            recip(ssq_[4 + blk:5 + blk], ssq_[4 + blk:5 + blk])
            ts(xs_[blk], xb.all(), ssq_[4 + blk:5 + blk], None, ALU.mult)
        for f in range(16):
            bk = bankb[f % 2]
            for blk in range(nblk):
                tr(bk[blk * 128:(blk + 1) * 128], xs_[blk, f * 128:(f + 1) * 128], ident_b.all())
            if f % 2 == 0:
                act(dstT(f), bk[0:w], AF.Identity, scale=afun(f), bias=shfun(f))
            else:
                ts(dstT(f), bk[0:w], afun(f), shfun(f), ALU.mult, ALU.add)

    for it in range(0 if stop > 1.9 else 4, 8 if stop > 1.9 else 5):
        r0 = it * 512
        dma(cosb.all(), dreg("cs", cs_in[0, :, r0:r0 + 512]))
        dma(sinb.all(), dreg("cs", cs_in[1, :, r0:r0 + 512]))
        norm_transpose(r0, 4, lambda f: hT[f], 512, a1, sh1, xs, xt, ssq, xk, "xk")
        if r0 + 512 > OWN0:
            c0 = max(0, OWN0 - r0)
            o0 = r0 + c0 - OWN0
            dma(dreg("hT_d", hT_d.rearrange("f p t -> p f t")[:, :, o0:o0 + 512 - c0]), hT[:, c0:512])
        if stop == 1.1:
            break
        for c in range(4):
            bk = bank[nbank()]
            for kc in range(16):
                mm(bk.all(), wkv[kc, c * 128:(c + 1) * 128], hT[kc], kc == 0, kc == 15)
            cp(ckv_sb[c], bk.all(), "dve")
            act(sq[c], bk.all(), AF.Square)
        if stop == 1.15:
            break
        lat_norm(4, ckv_sb, sq, gkv, lambda c: ckvn[c], 512, bank[7].all(), rstd_b.all(), 1.0 / 512)
        if stop == 1.2:
            break
        bk = bank[nbank()].parts(0, 64)
        for kc in range(16):
            mm(bk.all(), wkv[kc, 512:576], hT[kc], kc == 0, kc == 15)
        rope(bk.all(), 512, cosb.all(), sinb.all(), krope[r0:r0 + 512],
             (kr_b.all(), rt1.all(), rt2.all(), bank[nbank()].parts(0, 64).all()))
        if stop == 1.3:
            break
        if r0 + 512 > 14 * 128:
            c0 = max(0, 14 * 128 - r0)
            s0 = r0 + c0 - 14 * 128
            for c in range(2):
                bk = bank[nbank()]
                for kc in range(16):
                    mm(bk.all(), wkv[kc, 576 + c * 128:576 + (c + 1) * 128], hT[kc], kc == 0, kc == 15)
                cp(ks_st[c], bk.all(), evac_eng())
            for kvh in range(4):
                src = ks_st.parts((kvh % 2) * 64, (kvh % 2) * 64 + 64)[kvh // 2, c0:512]
                for dup in range(2):
                    dma(dreg("ks_d", ks_d[kvh, dup * 64:(dup + 1) * 64, s0:s0 + 512 - c0]), src)
            for blk in range(c0 // 128, 4):
                bk = bank[nbank()]
                for kc in range(16):
                    mm(bk[0:256], hT[kc, blk * 128:(blk + 1) * 128], wkv[kc, 832:1088], kc == 0, kc == 15)
                vv = vs_st[blk % 2]
                bk4 = PSA.view(bk.boff, [4, 64], F32)
                for dup in range(2):
                    cp(vv[:, dup, :], bk4.all(), evac_eng())
                sb = (r0 + blk * 128) // 128 - 14
                dma(dreg("vs_d", vs_d[sb]), vv.all())
        if stop == 1.4:
            break
        for h in range(16):
            bk = bank[nbank()]
            for c in range(4):
                mm(bk.all(), wukv[c, h * 256:h * 256 + 128], ckvn[c], c == 0, c == 3)
            ks_ = kst[(h // 4) % 2]
            cp(ks_[h % 4], bk.all(), evac_eng())
            if h % 4 == 3:
                dma(dreg("K_d", K_d[h - 3:h + 1].rearrange("h p t -> p h t")[:, :, r0:r0 + 512]), ks_.all())
        if stop == 1.5:
            break
        for blk in range(4):
            vs_ = vst[blk % 2]
            for hq in range(4):
                bk = bank[nbank()]
                for c in range(4):
                    mm(bk.all(), ckvn[c, blk * 128:(blk + 1) * 128], wukv4[c, hq * 4:(hq + 1) * 4, 128:256], c == 0, c == 3)
                cp(vs_[hq * 512:(hq + 1) * 512], bk.all(), evac_eng())
            dma(dreg("V_d", V_d[it * 4 + blk]), vs_.all())
    if dbg:
        dma(dreg("kr_d", kr_d), krope.all())
    if stop < 2:
        return finish(nc, es, P, locals())

    def mmg(o, l, r, start, stop_):
        P.op("pe", lambda e: e.matmul(o.ap, l.ap, r.ap, start=start, stop=stop_, skip_group_check=True), [l, r], [o])

    es3 = A.view(esink_full.boff, [16, 2, 128], BF16)
    wuqv = w_uq.rearrange("(kc p) n -> p kc n", p=128)
    wov = w_o.rearrange("(kc p) n -> p kc n", p=128)
    wupv = w_up.rearrange("(kc p) n -> p kc n", p=128)
    wdnv = w_down.rearrange("(kc p) n -> p kc n", p=128)
    hTdv = hT_d.rearrange("f p t -> p f t")
    gadv = ga_d.rearrange("f p t -> p f t")

    for st, (b0, b1) in enumerate(ST_BLOCKS):
        Wst = (b1 - b0) * 128
        o0 = b0 * 128
        tw = 384 if Wst % 384 == 0 else 512
        tiles = [(t, tw) for t in range(0, Wst, tw)]
        ph.reset()
        h2off = ARENA_BYTES - 16 * Wst * 2
        h2T = A.view(h2off, [16, Wst], BF16)
        mark_ffn = ph.o
        gated = ph.get([16, Wst], BF16)
        mark_g = ph.o
        hTs = ph.get([16, Wst], BF16)
        panA = [ph.get([16, 512], BF16) for _ in range(2)]
        mark_p2 = ph.o
        cosq = ph.get([Wst], F32, 0, 64)
        sinq = ph.get([Wst], F32, 0, 64)
        cqn = ph.get([6, Wst], BF16)
        mark_q = ph.o
        cq_sb = ph.get([6, 512], F32)
        sq6 = ph.get([6, 512], BF16)
        rstd6 = ph.get([512], F32)
        ph.o = mark_q
        uqp = [ph.get([6, 192], BF16) for _ in range(2)]
        qn_st = [ph.get([Wst], BF16) for _ in range(2)]
        qr_st = [ph.get([Wst], BF16, 0, 64) for _ in range(2)]
        qkb = [ph.get([512], BF16, 0, 64) for _ in range(2)]
        qt1 = [ph.get([512], F32, 0, 64) for _ in range(2)]
        qt2 = [ph.get([512], F32, 0, 64) for _ in range(2)]
        dma(hTs.all(), dreg("hT_d", hTdv[:, :, o0:o0 + Wst]))
        dma(cosq.all(), dreg("cs", cs_in[0, :, OWN0 + o0:OWN0 + o0 + Wst]))
        dma(sinq.all(), dreg("cs", cs_in[1, :, OWN0 + o0:OWN0 + o0 + Wst]))
        dma(panA[0].all(), dreg("w_in", winv[:, :, 0:512]), "pool")
        dma(panA[1][:, 0:256], dreg("w_in", winv[:, :, 512:768]), "pool")
        for (t0, w) in tiles:
            for c in range(6):
                bk = bank[nbank()]
                for kc in range(16):
                    mm(bk[0:w], panA[c // 4][kc, (c % 4) * 128:(c % 4 + 1) * 128], hTs[kc, t0:t0 + w], kc == 0, kc == 15)
                cp(cq_sb[c, 0:w], bk[0:w], "dve")
                act(sq6[c, 0:w], bk[0:w], AF.Square)
            lat_norm(6, cq_sb, sq6, gq, lambda c: cqn[c, t0:t0 + w], w, bank[7][0:w], rstd6[0:w], 1.0 / 768)
        qitems = [(h, ti, t0, w) for h in range(16) for ti, (t0, w) in enumerate(tiles)]

        def q_s1(i):
            h, ti, t0, w = qitems[i]
            up = uqp[h % 2]
            if ti == 0:
                dma(up.all(), dreg("w_uq", wuqv[:, :, h * 192:(h + 1) * 192]), "pool")
            bk = bank[i % 2]
            for kc in range(6):
                mm(bk[0:w], up[kc, 0:128], cqn[kc, t0:t0 + w], kc == 0, kc == 5)
            bk2 = bank[2 + i % 2].parts(0, 64)
            for kc in range(6):
                mm(bk2[0:w], up[kc, 128:192], cqn[kc, t0:t0 + w], kc == 0, kc == 5)

        def q_s2(i):
            h, ti, t0, w = qitems[i]
            cp(qn_st[h % 2][t0:t0 + w], bank[i % 2][0:w], evac_eng())
            bk2 = bank[2 + i % 2].parts(0, 64)
            rope(bk2[0:w], w, cosq[t0:t0 + w], sinq[t0:t0 + w], qr_st[h % 2][t0:t0 + w],
                 (qkb[i % 2][0:w], qt1[i % 2][0:w], qt2[i % 2][0:w], bank[4 + i % 2].parts(0, 64)[0:w]))
            if ti == len(tiles) - 1:
                dma(dreg("qn_d", qn_d[h, :, o0:o0 + Wst]), qn_st[h % 2].all())
                dma(dreg("qr_d", qr_d[h, :, o0:o0 + Wst]), qr_st[h % 2].all())

        q_s1(0)
        for i in range(len(qitems)):
            if i + 1 < len(qitems):
                q_s1(i + 1)
            q_s2(i)
        if stop == 2.1:
            break
        ph.o = mark_p2
        nsb = (b1 - b0) + 1
        EBh = [ph.get([2, 4, 2, 128], BF16) for _ in range(2)]
        ebst = [ph.get([8 * 128], F32) for _ in range(1)]
        ks_sw = ph.get([4, nsb * 128], BF16)
        vs_sw = ph.get([nsb, 512], BF16)
        qsz = [ph.get([4, Wst], BF16) for _ in range(2)]
        ptraw = [ph.get([4, 128], F32) for _ in range(2)] * 2
        ptb = [ph.get([4, 128], BF16) for _ in range(4)]
        rden = [ph.get([512], F32) for _ in range(2)]
        _z0 = qsz[0].parts(64, 128).all()
        _z1 = qsz[1].parts(0, 64).all()
        P.op("pool", lambda e, _z0=_z0: e.memset(_z0.ap, 0.0), [], [_z0])
        P.op("pool", lambda e, _z1=_z1: e.memset(_z1.ap, 0.0), [], [_z1])
        for kvh in range(4):
            dma(ks_sw[kvh], dreg("ks_d", ks_d[kvh, :, b0 * 128:(b0 + nsb) * 128]))
        dma(vs_sw.all(), dreg("vs_d", vs_d[b0:b0 + nsb].rearrange("s p c -> p s c")))
        for hk in range(4):
            pn = panA[hk % 2]
            dma(pn.all(), dreg("w_in", winv[:, :, 1344 + hk * 512:1344 + (hk + 1) * 512]), "pool")
            EBc = EBh[hk % 2]
            EBcf = A.view(EBc.boff, [2, 8 * 128], BF16)
            for pi in range(2):
                dma(ebst[0].all(), dreg("swab", swab[pi, :, hk * 1024:(hk + 1) * 1024]))
                act(EBcf[pi], ebst[0].all(), AF.Exp)
            for (t0, w) in tiles:
                for c in range(4):
                    bk = bank[nbank()]
                    for kc in range(16):
                        mm(bk[0:w], pn[kc, c * 128:(c + 1) * 128], hTs[kc, t0:t0 + w], kc == 0, kc == 15)
                    cp(qsz[0].parts(0, 64)[c, t0:t0 + w], bk.parts(0, 64)[0:w], "act")
                    cp(qsz[1].parts(64, 128)[c, t0:t0 + w], bk.parts(64, 128)[0:w], "dve")
            chains = [(b, e) for b in range(b0, b1) for e in range(2)]

            def swa_S(i):
                b, e = chains[i]
                lb = b - b0
                for pi in range(2):
                    sbl = lb + pi
                    S = PSA.view(bank[2 * (i % 2) + pi].boff, [4, 128], F32)
                    mm(S.all(), ks_sw[hk, sbl * 128:(sbl + 1) * 128], qsz[e][:, lb * 128:(lb + 1) * 128])

            swa_S(0)
            for i, (b, e) in enumerate(chains):
                lb = b - b0
                if i + 1 < len(chains):
                    swa_S(i + 1)
                for pi in range(2):
                    S = PSA.view(bank[2 * (i % 2) + pi].boff, [4, 128], F32)
                    gsb = b + pi
                    pr = ptraw[(i % 2) * 2 + pi]
                    act(pr.all(), S.all(), AF.Exp, scale=SCALE_SWA, bias=(bias_oth if gsb <= 1 else None))
                    tt(ptb[(i % 2) * 2 + pi].all(), pr.all(), EBc[pi, :, e, :], ALU.mult)
                ob = PSA.view(bank[4 + 2 * (i % 2)].boff, [4, 128], F32)
                db = PSA.view(bank[5 + 2 * (i % 2)].boff, [4, 128], F32)
                for pi in range(2):
                    sbl = lb + pi
                    mmg(ob.all(), vs_sw[sbl, hk * 128:(hk + 1) * 128], ptb[(i % 2) * 2 + pi].all(), pi == 0, pi == 1)
                for pi in range(2):
                    mmg(db.all(), ones_b.all(), ptb[(i % 2) * 2 + pi].all(), pi == 0, False)
                mmg(db.all(), onepad.all(), es3[4 * hk:4 * hk + 4, e, :], False, True)
                rd4 = A.view(rden[i % 2].boff, [4, 128], F32)
                act(rd4.all(), db.all(), AF.Ln)
                act(rd4.all(), rd4.all(), AF.Exp, scale=-1.0)
                tt(gated.parts(e * 64, e * 64 + 64)[4 * hk:4 * hk + 4, lb * 128:(lb + 1) * 128],
                   ob.parts(e * 64, e * 64 + 64).all(), rd4.parts(e * 64, e * 64 + 64).all(), ALU.mult)
        if stop == 2.2:
            break
        ph.o = mark_p2
        sig = ph.get([512], F32)
        ga_st = [ph.get([4, Wst], BF16) for _ in range(2)]
        for gi in (4, 5, 6, 7, 0, 1, 2, 3):
            pn = panA[gi % 2]
            dma(pn.all(), dreg("w_in", winv[:, :, 3904 + gi * 512:3904 + (gi + 1) * 512]), "pool")
            for (t0, w) in tiles:
                for c in range(4):
                    ch = (gi % 4) * 4 + c
                    bk = bank[nbank()]
                    for kc in range(16):
                        mm(bk[0:w], pn[kc, c * 128:(c + 1) * 128], hTs[kc, t0:t0 + w], kc == 0, kc == 15)
                    if gi >= 4:
                        act(sig[0:w], bk[0:w], AF.Sigmoid)
                        tt(gated[ch, t0:t0 + w], sig[0:w], gated[ch, t0:t0 + w], ALU.mult)
                    else:
                        act(ga_st[gi % 2][c, t0:t0 + w], bk[0:w], AF.Sigmoid)
            if gi < 4:
                dma(dreg("ga_d", gadv[:, gi * 4:gi * 4 + 4, o0:o0 + Wst]), ga_st[gi % 2].all())
        if stop == 2.3:
            break
        ph.o = mark_g
        Kb = [ph.get([4096], BF16) for _ in range(2)]
        Vb = [ph.get([32, 256], BF16) for _ in range(2)]
        qn = [ph.get([Wst], BF16) for _ in range(2)]
        qr = [ph.get([Wst], BF16) for _ in range(2)]
        for _q in qr:
            P.op("pool", lambda e, _q=_q: e.memset(_q.parts(64, 128).all().ap, 0.0), [], [_q.parts(64, 128).all()])
        ga = [ph.get([Wst], BF16) for _ in range(2)]
        pt = [ph.get([512], BF16) for _ in range(8)]
        rdm = [ph.get([512], F32) for _ in range(2)]
        otmp = [ph.get([512], F32) for _ in range(2)]
        pacc = [[ph.get([512], F32) for _ in range(3)] for _ in range(2)]
        phi = [ph.get([512], BF16) for _ in range(2)]
        plo = [ph.get([512], BF16) for _ in range(2)]
        nkb = NOTH + b1
        if st == 0:
            apan = [ph.get([16, 256], BF16) for _ in range(2)]
            rowv = ph.get([D], F32, 0, 1)
            grow1 = ph.get([D], F32, 0, 1)
            ada_bank = bank[7]

            def ada_step(k):
                v, q = 2 + k // 8, k % 8
                if q == 0:
                    dma(rowv.all(), dreg("b_ada", b_ada[:, v * D:(v + 1) * D]))
                    gi_ = {2: 1, 4: 2, 5: 3}.get(v)
                    if gi_ is not None:
                        dma(grow1.all(), dreg("grow", grow[gi_:gi_ + 1, :]))
                ap_ = apan[k % 2]
                dma(ap_.all(), dreg("w_ada", wav[:, :, v * D + q * 256:v * D + (q + 1) * 256]), "pool")
                bk = ada_bank.parts(0, 1)
                for kc in range(16):
                    mm(bk[0:256], cTb[kc:kc + 1], ap_[kc], kc == 0, kc == 15)
                tt(rowv[q * 256:(q + 1) * 256], bk[0:256], rowv[q * 256:(q + 1) * 256], ALU.add)
                if q == 7:
                    if v in (2, 5):
                        tt(rowv.all(), rowv.all(), grow1.all(), ALU.mult)
                        dst = r1_b if v == 2 else r2_b
                        for nb in range(4):
                            mm(ada_bank.all(), one_row.all(), rowv[nb * 512:(nb + 1) * 512])
                            cp(dst[nb * 512:(nb + 1) * 512], ada_bank.all(), "dve")
                    else:
                        if v == 4:
                            stt(rowv.all(), rowv.all(), 1.0, grow1.all(), ALU.add, ALU.mult)
                        for f in range(16):
                            mm(ada_bank[f:f + 1], rowv[f * 128:(f + 1) * 128], one_row[0:1])
                        m0 = 48 if v == 3 else 32
                        cp(modv[m0:m0 + 16], ada_bank[0:16], "dve")
        ada_k = 0
        items = []
        for h in range(16):
            for ti, (t0, w) in enumerate(tiles):
                qb0 = b0 + t0 // 128
                nb_ = w // 128
                kl = [(kb, 0, False) for kb in range(NOTH + qb0)] + [(NOTH + qb0 + s_, s_ * 128, True) for s_ in range(nb_)]
                for idx, (kb, c0, diag) in enumerate(kl):
                    items.append((h, ti, t0, w, kb, c0, diag, idx == 0, idx == len(kl) - 1))

        def mla_loads(h):
            dma(Kb[h % 2][0:nkb * 128], dreg("K_d", K_d[h, :, 0:nkb * 128]))
            if h % 2 == 0:
                dma(Vb[(h // 2) % 2][0:nkb], dreg("V_d", V_d[0:nkb, :, (h // 2) * 256:(h // 2 + 1) * 256].rearrange("k p c -> p k c")))
            dma(qn[h % 2].all(), dreg("qn_d", qn_d[h, :, o0:o0 + Wst]))
            dma(qr[h % 2].parts(0, 64).all(), dreg("qr_d", qr_d[h, :, o0:o0 + Wst]))
            dma(ga[h % 2].all(), dreg("ga_d", ga_d[h, :, o0:o0 + Wst]))

        def mla_S(i):
            h, ti, t0, w, kb, c0, diag, first, last = items[i]
            if i == 0 or items[i - 1][0] != h:
                mla_loads(h)
            S = bank[4 + i % nSb]
            mmg(S[c0:w], Kb[h % 2][kb * 128:(kb + 1) * 128], qn[h % 2][t0 + c0:t0 + w], True, False)
            mmg(S[c0:w], krope_full[kb * 128:(kb + 1) * 128], qr[h % 2][t0 + c0:t0 + w], False, True)

        nSb = 3 if st == 0 else 4
        LA = nSb - 1
        nS = 0
        tcount = 0
        for i, (h, ti, t0, w, kb, c0, diag, first, last) in enumerate(items):
            if st == 0 and i % 30 == 15 and ada_k < 32:
                ada_step(ada_k)
                ada_k += 1
            while nS <= min(i + LA, len(items) - 1):
                mla_S(nS)
                nS += 1
            S = bank[4 + i % nSb]
            p = pt[i % 8]
            act(p[c0:w], S[c0:w], AF.Exp, scale=SCALE_MLA, bias=(bias_oth if kb <= NOTH else None))
            if diag:
                tt(p[c0:c0 + 128], p[c0:c0 + 128], tri_b.all(), ALU.mult)
            oacc, dacc = bank[2 * (tcount % 2)], bank[2 * (tcount % 2) + 1]
            vb_ = Vb[(h // 2) % 2]
            mmg(oacc[c0:w], vb_[kb, (h % 2) * 128:(h % 2 + 1) * 128], p[c0:w], first, last)
            if first:
                tidx = 0
            ai = tidx % 2
            pa_ = pacc[tcount % 2][ai]
            aeng = "dve"
            if tidx < 2:
                assert c0 == 0
                cp(pa_[0:w], p[0:w], aeng)
            else:
                tt(pa_[c0:w], pa_[c0:w], p[c0:w], ALU.add, eng=aeng)
            tidx += 1
            if last:
                rd_ = rdm[tcount % 2]
                ot_ = otmp[tcount % 2]
                hi_, lo_ = phi[tcount % 2], plo[tcount % 2]
                pa_ = pacc[tcount % 2][0]
                tt(pa_[0:w], pa_[0:w], pacc[tcount % 2][1][0:w], ALU.add)
                cp(hi_[0:w], pa_[0:w], "dve")
                tt(lo_[0:w], pa_[0:w], hi_[0:w], ALU.subtract)
                mmg(dacc[0:w], ones_b.all(), hi_[0:w], True, False)
                mmg(dacc[0:w], ones_b.all(), lo_[0:w], False, True)
                act(rd_[0:w], dacc[0:w], AF.Ln, bias=tinyc)
                act(rd_[0:w], rd_[0:w], AF.Exp, scale=-1.0)
                tt(ot_[0:w], oacc[0:w], rd_[0:w], ALU.mult)
                tt(ot_[0:w], ot_[0:w], ga[h % 2][t0:t0 + w], ALU.mult, eng="pool")
                tt(gated[h, t0:t0 + w], ot_[0:w], gated[h, t0:t0 + w], ALU.add, eng="pool")
                tcount += 1
        while st == 0 and ada_k < 32:
            ada_step(ada_k)
            ada_k += 1
        if dbg:
            dma(dreg("gt_d", gt_d.rearrange("f p t -> p f t")[:, :, o0:o0 + Wst]), gated.all())
        if stop == 2.4:
            break
        ph.o = mark_g
        ph.limit = h2off
        wo = ph.get([16, 2048], BF16)
        xb = ph.get([2048], F32)
        x1 = ph.get([2048], F32)
        xs2 = ph.get([2048], BF16)
        ss = ph.get([8], F32)
        for n in range(4):
            dma(wo[:, n * 512:(n + 1) * 512], dreg("w_o", wov[:, :, n * 512:(n + 1) * 512]), "pool")
        junkw = ph.get([2048], BF16)

        def wo_mm(b):
            lb = b - b0
            for n in range(4):
                for kc in range(16):
                    mm(big[lb % 2][n * 512:(n + 1) * 512], gated[kc, lb * 128:(lb + 1) * 128], wo[kc, n * 512:(n + 1) * 512], kc == 0, kc == 15)

        wo_mm(b0)
        for b in range(b0, b1):
            lb = b - b0
            if b + 1 < b1:
                wo_mm(b + 1)
            bg = big[lb % 2]
            dma(xb.all(), dreg("xk", xk[OWN0 + b * 128:OWN0 + (b + 1) * 128, :]))
            act(junkw.all(), bg.all(), AF.Square, accum=ss[0:1])
            act(ss[1:2], ss[0:1], AF.Sqrt, scale=1.0 / D, bias=epsc)
            recip(ss[1:2], ss[1:2])
            stt(x1.all(), bg.all(), ss[1:2], r1_b.all(), ALU.mult, ALU.mult)
            tt(x1.all(), x1.all(), xb.all(), ALU.add)
            if b >= 1:
                dma(dreg("x1_d", x1_d[(b - 1) * 128:b * 128, :], (b - 1) * 128, b * 128), x1.all())
            act(junkw.all(), x1.all(), AF.Square, accum=ss[2:3])
            act(ss[3:4], ss[2:3], AF.Sqrt, scale=1.0 / D, bias=epsc)
            recip(ss[3:4], ss[3:4])
            ts(xs2.all(), x1.all(), ss[3:4], None, ALU.mult)
            for f in range(16):
                bk = bankb[(lb % 2) * 4 + f // 4]
                tr(bk[(f % 4) * 128:(f % 4 + 1) * 128], xs2[f * 128:(f + 1) * 128], ident_b.all())
            for f in range(16):
                bk = bankb[(lb % 2) * 4 + f // 4]
                if f % 2 == 0:
                    act(h2T[f, lb * 128:(lb + 1) * 128], bk[(f % 4) * 128:(f % 4 + 1) * 128], AF.Identity, scale=a2(f), bias=sh2(f))
                else:
                    ts(h2T[f, lb * 128:(lb + 1) * 128], bk[(f % 4) * 128:(f % 4 + 1) * 128], a2(f), sh2(f), ALU.mult, ALU.add)
        if stop == 2.5:
            break
        ph.o = mark_ffn
        aT = ph.get([44, 1024], BF16)
        mark_dn = ph.o
        upan = [ph.get([16, 256], BF16) for _ in range(2)]
        U = [[ph.get([tw + 2], F32) for _ in range(2)] for _ in range(2)]
        c1 = ph.get([512], F32)
        cx = [ph.get([512], F32) for _ in range(2)]
        sg = ph.get([512], F32)
        hal = 128 if st == 0 else 0
        for c in range(44):
            up = upan[c % 2]
            dma(up[:, 0:128], dreg("w_up", wupv[:, :, c * 128:(c + 1) * 128]), "pool")
            dma(up[:, 128:256], dreg("w_up", wupv[:, :, (44 + c) * 128:(45 + c) * 128]), "pool")
            for ti, (t0, w) in enumerate(tiles):
                for br in range(2):
                    cc = c + 44 * br
                    bk = bank[2 * (ti % 2) + br]
                    for kc in range(16):
                        mm(bk[0:w], up[kc, br * 128:(br + 1) * 128], h2T[kc, t0:t0 + w], kc == 0, kc == 15)
                    Ub = U[ti % 2][br]
                    cp(Ub[2:2 + w], bk[0:w], "act")
                    if ti == 0:
                        if st == 0:
                            P.op("dve", lambda e, Ub=Ub: e.memset(Ub[0:2].ap, 0.0), [], [Ub[0:2]])
                        else:
                            cp(Ub[0:2], carry[cc], "dve")
                    else:
                        cp(Ub[0:2], U[(ti - 1) % 2][br][w:w + 2], "dve")
                    if st == 0 and ti == 0:
                        ts(Ub[128:130], Ub[128:130], flag, None, ALU.mult)
                    ts(c1[0:w], Ub[0:w], cw(cc, 0), cb(cc), ALU.mult, ALU.add)
                    stt(c1[0:w], Ub[1:w + 1], cw(cc, 1), c1[0:w], ALU.mult, ALU.add)
                    stt(cx[br][0:w], Ub[2:w + 2], cw(cc, 2), c1[0:w], ALU.mult, ALU.add)
                    if ti == len(tiles) - 1:
                        cp(carry[cc], Ub[w:w + 2], "dve")
                act(sg[0:w], cx[0][0:w], AF.Silu)
                off = hal if ti == 0 else 0
                tt(aT[c, t0 + off - hal:t0 + w - hal], sg[off:w], cx[1][off:w], ALU.mult)
        if stop == 2.6:
            break
        ph.o = mark_dn
        ph.limit = ARENA_BYTES
        ytok = ph.get([4, 2048], F32)
        dpan = [ph.get([44, 128], BF16) for _ in range(2)]
        ysb = [ph.get([512], F32) for _ in range(2)]
        x1b = ph.get([2048], F32)
        junk = ph.get([2048], BF16)
        ss2 = ph.get([8], F32)
        for dt_ in range(2):
            a0 = dt_ * 512
            for f in range(16):
                dp = dpan[f % 2]
                dma(dp.all(), dreg("w_down", wdnv[:, :, f * 128:(f + 1) * 128]), "pool")
                bk = bank[f % 2]
                for kc in range(44):
                    mm(bk.all(), dp[kc], aT[kc, a0:a0 + 512], kc == 0, kc == 43)
                cp(ysb[f % 2].all(), bk.all(), evac_eng())
                tb = bank[2 + f % 2]
                for blk in range(4):
                    tr(tb[blk * 128:(blk + 1) * 128], ysb[f % 2][blk * 128:(blk + 1) * 128], ident_f.all())
                tb4 = PSA.view(tb.boff, [4, 128], F32)
                cp(ytok[:, f * 128:(f + 1) * 128], tb4.all(), evac_eng())
            for blk in range(4):
                row0 = st * 1024 + a0 + blk * 128
                dma(x1b.all(), dreg("x1_d", x1_d[row0:row0 + 128, :], row0, row0 + 128))
                act(junk.all(), ytok[blk], AF.Square, accum=ss2[0:1])
                act(ss2[1:2], ss2[0:1], AF.Sqrt, scale=1.0 / D, bias=epsc)
                recip(ss2[1:2], ss2[1:2])
                stt(ytok[blk], ytok[blk], ss2[1:2], r2_b.all(), ALU.mult, ALU.mult)
                tt(ytok[blk], ytok[blk], x1b.all(), ALU.add)
                dma(dreg("out", out[row0:row0 + 128, :], row0, row0 + 128), ytok[blk])
    if dbg:
        dbg_mod = nc.dram_tensor("dbg_mod", [128, 64], F32, kind="ExternalOutput").ap()
        dbg_r = nc.dram_tensor("dbg_r", [128, 4096], F32, kind="ExternalOutput").ap()
        dma(dreg("dbg_mod", dbg_mod), modv.all())
        dma(dreg("dbg_r", dbg_r[:, 0:2048]), r1_b.all())
        dma(dreg("dbg_r", dbg_r[:, 2048:4096]), r2_b.all())
    return finish(nc, es, P, locals())


def finish(nc, es, P, L):
    out = L["out"]
    P.wait_events("sp", list(P.all_dma_events))
    sems = {"e": {}, "d": {}}
    for e in P.ENG:
        sems["e"][e] = es.enter_context(nc.semaphore("s_" + e))
    for key in P.dma_cnt:
        sems["d"][key] = es.enter_context(nc.semaphore("d_%s_%d" % key))
    with nc.Block() as block:
        P.emit(block, sems)
    es.close()
    return nc


def _t5_bucket(dist):
    max_exact = 16
    n = np.maximum(dist, 0)
    large = max_exact + (np.log(np.maximum(n, 1).astype(np.float32) / max_exact)
                         / np.float32(math.log(128 / max_exact)) * (32 - max_exact)).astype(np.int32)
    large = np.minimum(large, 31)
    return np.where(n < max_exact, n, large)


def _consts(j):
    cst = np.zeros((128, 512), np.float32)
    cst[:, 0:128] = np.eye(128, dtype=np.float32)
    cst[:, 128:256] = 1.0
    k = np.arange(128)[:, None]
    q = np.arange(128)[None, :]
    cst[:, 256:384] = (k <= q).astype(np.float32)
    Rm = np.zeros((64, 64), np.float32)
    for d in range(32):
        Rm[d + 32, d] = -1.0
        Rm[d, d + 32] = 1.0
    cst[0:64, 384:448] = Rm
    cst[:, 448] = EPS
    cst[:, 449] = -30000.0 if j == 0 else 0.0
    cst[:, 450] = 0.0 if j == 0 else 1.0
    cst[:, 451] = 1e-30
    return cst


def _rope_tab(j):
    r = np.arange(4096)
    pos = (r if j == 1 else np.maximum(r - 2048, 0)).astype(np.float32)
    inv = (np.float32(10000.0) ** (-np.arange(0, 64, 2, dtype=np.float32) / np.float32(64))).astype(np.float32)
    ang = pos[:, None] * inv[None, :]
    ang = np.concatenate([ang, ang], axis=-1)
    return np.stack([np.cos(ang).T, np.sin(ang).T]).astype(np.float32)


def prep_inputs(inp):
    f = lambda a: np.ascontiguousarray(np.asarray(a, dtype=np.float32))
    x = f(inp["x"])
    rel_bias = f(inp["rel_bias"])
    k = np.arange(128)[:, None]
    q = np.arange(128)[None, :]
    bt = np.zeros((2, 128, 32, 128), np.float32)
    d_prev = 128 + q - k
    d_cur = q - k
    bprev = rel_bias[_t5_bucket(d_prev)]
    bcur = rel_bias[_t5_bucket(d_cur)]
    bt[0] = np.where((k > q)[:, None, :], bprev.transpose(0, 2, 1), np.float32(-30000.0))
    bt[1] = np.where((k <= q)[:, None, :], bcur.transpose(0, 2, 1), np.float32(-30000.0))
    swab = bt.reshape(2, 128, 32 * 128)
    sinkrow = np.repeat(f(inp["sinks"])[0], 128)[None, :].astype(np.float32)
    vecs = np.zeros((128, 10 + 88 * 4), np.float32)
    vecs[:, 0:6] = f(inp["g_q_lat"])[0].reshape(6, 128).T
    vecs[:, 6:10] = f(inp["g_kv_lat"])[0].reshape(4, 128).T
    cwv = f(inp["conv_w"])[0]
    vecs[:, 10:10 + 264] = cwv.reshape(3, 88, 128).transpose(2, 1, 0).reshape(128, 264)
    vecs[:, 274:274 + 88] = f(inp["conv_b"])[0].reshape(88, 128).T
    grow = np.stack([f(inp["g_pre_mix"])[0], f(inp["g_post_mix"])[0], f(inp["g_pre_ffn"])[0], f(inp["g_post_ffn"])[0]])
    shared = dict(w_ada=f(inp["w_ada"])[0], b_ada=f(inp["b_ada"]), grow=grow, w_in=f(inp["w_in"])[0],
                  w_uq=f(inp["w_uq"])[0], w_ukv=f(inp["w_ukv"])[0], w_o=f(inp["w_o"])[0], w_up=f(inp["w_up"])[0],
                  w_down=f(inp["w_down"])[0], vecs=vecs, swab=swab, sinkrow=sinkrow)
    csts = [_consts(0), _consts(1)]
    ropes = [_rope_tab(0), _rope_tab(1)]
    zeros = np.zeros((2048, D), np.float32)
    maps = []
    for core in range(8):
        b, j = core // 2, core % 2
        xk = x[b] if j == 1 else np.concatenate([zeros, x[b][:2048]], axis=0)
        m = dict(shared)
        m.update(xk=np.ascontiguousarray(xk), c=f(inp["c"])[b:b + 1], cst=csts[j], cs=ropes[j])
        maps.append(m)
    return maps


_NC_CACHE = {}


def kernel(**inputs):
    if "nc" not in _NC_CACHE:
        _NC_CACHE["nc"] = build(False)
    nc = _NC_CACHE["nc"]
    maps = prep_inputs(inputs)
    res = run_bass_kernel_spmd(nc, maps, core_ids=list(range(8)))
    outp = np.zeros((4, 4096, D), np.float32)
    for core in range(8):
        b, j = core // 2, core % 2
        outp[b, j * 2048:(j + 1) * 2048] = res.results[core]["out"]
    return outp
```

```python
import math
import numpy as np
import concourse.bass as bass
import concourse.mybir as mybir

F32 = mybir.dt.float32
BF16 = mybir.dt.bfloat16
AF = mybir.ActivationFunctionType
ALU = mybir.AluOpType
ISZ = {F32: 4, BF16: 2}


class V:
    def __init__(self, ap, space, p0, p1, b0, b1):
        self.ap, self.space, self.p0, self.p1, self.b0, self.b1 = ap, space, p0, p1, b0, b1

    def reg(self):
        return (self.space, self.p0, self.p1, self.b0, self.b1)


class Arena:
    def __init__(self, ap, space, nbytes):
        self.ap = ap
        self.space = space
        self.nbytes = nbytes

    def view(self, boff, shape, dt, p0=0, np_=128):
        return Tn(self, boff, list(shape), dt, p0, np_)


class Tn:
    def __init__(self, arena, boff, shape, dt, p0, np_):
        self.arena, self.boff, self.shape, self.dt, self.p0, self.np_ = arena, boff, shape, dt, p0, np_
        isz = ISZ[dt]
        n = int(np.prod(shape))
        self.nbytes = n * isz
        assert boff % 4 == 0 and (n * isz) % 4 == 0, (boff, n, isz)
        assert boff + self.nbytes <= arena.nbytes, (boff, self.nbytes, arena.nbytes)
        a = arena.ap[p0:p0 + np_, boff // 4:(boff + n * isz) // 4]
        if dt != F32:
            a = a.bitcast(dt)
        if len(shape) == 2:
            a = a.rearrange("p (a b) -> p a b", a=shape[0])
        elif len(shape) == 3:
            a = a.rearrange("p (a b c) -> p a b c", a=shape[0], b=shape[1])
        elif len(shape) == 4:
            a = a.rearrange("p (a b c d) -> p a b c d", a=shape[0], b=shape[1], c=shape[2])
        self.full = a
        st = [isz]
        for d in reversed(shape[1:]):
            st.insert(0, st[0] * d)
        self.strides = st

    def __getitem__(self, idx):
        if not isinstance(idx, tuple):
            idx = (idx,)
        idx = list(idx) + [slice(None)] * (len(self.shape) - len(idx))
        lo = self.boff
        hi = self.boff
        apidx = [slice(None)]
        p0, p1 = self.p0, self.p0 + self.np_
        for d, (i, s, n) in enumerate(zip(idx, self.strides, self.shape)):
            if isinstance(i, int):
                assert 0 <= i < n, (i, n)
                lo += i * s
                hi += i * s
                apidx.append(i)
            else:
                a = 0 if i.start is None else i.start
                b = n if i.stop is None else i.stop
                assert 0 <= a < b <= n, (a, b, n, self.shape, idx)
                lo += a * s
                hi += (b - 1) * s
                apidx.append(slice(a, b))
        hi += ISZ[self.dt]
        return V(self.full[tuple(apidx)], self.arena.space, p0, p1, lo, hi)

    def parts(self, q0, q1):
        t = Tn.__new__(Tn)
        t.__dict__.update(self.__dict__)
        t.p0 = self.p0 + q0
        t.np_ = q1 - q0
        idx = [slice(q0, q1)] + [slice(None)] * len(self.shape)
        t.full = self.full[tuple(idx)]
        return t

    def all(self):
        return self[tuple([slice(None)] * len(self.shape))]


def DV(ap, name, lo=0, hi=1 << 40):
    return V(ap, "d:" + name, 0, 1, lo, hi)


class Prog:
    ENG = ["pe", "act", "dve", "pool", "sp"]

    def __init__(self, nc, n_dma_sems=8):
        self.nc = nc
        self.ops = {e: [] for e in self.ENG}
        self.recs = {}
        self.waited = {e: {} for e in self.ENG}
        self.n_dma_sems = n_dma_sems
        self.dma_rr = {e: 0 for e in self.ENG}
        self.dma_cnt = {}
        self.all_dma_events = []

    def _add_wait(self, eng, op, ev):
        if ev is None:
            return
        if ev[0] == "e":
            key = ("e", ev[1])
            val = ev[2]
            self.ops[ev[1]][ev[2]]["sig"] = True
        else:
            key = ("d", ev[1])
            val = ev[2]
        w = self.waited[eng]
        if w.get(key, -1) >= val:
            return
        w[key] = val
        op["waits"].append(ev)

    def _deps(self, eng, op, reads, writes, event):
        for v, is_w in [(r, False) for r in reads] + [(w, True) for w in writes]:
            if v.space == "ps":
                v = V(v.ap, "ps", 0, 128, v.b0 // 2048 * 2048, (v.b1 + 2047) // 2048 * 2048)
            lst = self.recs.setdefault(v.space, [])
            keep = []
            myidx = len(self.ops[eng])
            for r in lst:
                if r[4] == eng and r[5] == myidx:
                    keep.append(r)
                    continue
                overlap = not (r[1] <= v.p0 or v.p1 <= r[0] or r[3] <= v.b0 or v.b1 <= r[2])
                if overlap and (is_w or r[6] or (v.space == "ps" and r[4] != eng)):
                    if not (r[4] == "pe" and eng == "pe"):
                        self._add_wait(eng, op, r[7])
                if is_w and overlap and r[0] >= v.p0 and r[1] <= v.p1 and r[2] >= v.b0 and r[3] <= v.b1:
                    continue
                if (not is_w) and (not r[6]) and r[4] == eng and eng != "sp" and event[0] == "e" \
                        and r[0] == v.p0 and r[1] == v.p1 and r[2] == v.b0 and r[3] == v.b1:
                    continue
                keep.append(r)
            keep.append([v.p0, v.p1, v.b0, v.b1, eng, len(self.ops[eng]), is_w, event])
            self.recs[v.space] = keep

    def op(self, eng, fn, reads=(), writes=(), acc=False):
        o = {"fn": fn, "waits": [], "sig": False, "dma": None, "acc": acc}
        ev = ("e", eng, len(self.ops[eng]))
        self._deps(eng, o, reads, writes, ev)
        self.ops[eng].append(o)
        return ev

    def dma(self, eng, out, in_, extra_reads=(), extra_writes=()):
        k = self.dma_rr[eng] % self.n_dma_sems
        self.dma_rr[eng] += 1
        key = (eng, k)
        prev = self.dma_cnt.get(key, 0)
        o = {"waits": [], "sig": False, "dma": key, "acc": False}
        if prev > 0:
            self._add_wait(eng, o, ("d", key, prev))
        cnt = prev + 16
        self.dma_cnt[key] = cnt
        ev = ("d", key, cnt)
        oap, iap = out.ap, in_.ap
        o["fn"] = lambda e: e.dma_start(out=oap, in_=iap)
        self._deps(eng, o, [in_] + list(extra_reads), [out] + list(extra_writes), ev)
        self.ops[eng].append(o)
        self.all_dma_events.append(ev)
        return ev

    def wait_events(self, eng, events):
        o = {"fn": None, "waits": [], "sig": False, "dma": None, "acc": False}
        for ev in events:
            self._add_wait(eng, o, ev)
        self.ops[eng].append(o)

    def emit(self, block, sems):
        nc = self.nc
        cnts = {}
        for e in self.ENG:
            c = 0
            arr = []
            for o in self.ops[e]:
                if o["sig"]:
                    c += 1
                arr.append(c)
            cnts[e] = arr

        def run(e, engobj):
            for i, o in enumerate(self.ops[e]):
                for ev in o["waits"]:
                    if ev[0] == "e":
                        engobj.wait_ge(sems["e"][ev[1]], cnts[ev[1]][ev[2]])
                    else:
                        engobj.wait_ge(sems["d"][ev[1]], ev[2])
                if o["fn"] is None:
                    continue
                inst = o["fn"](engobj)
                if o["dma"] is not None:
                    inst.then_inc(sems["d"][o["dma"]], 16)
                elif o["sig"]:
                    inst.then_inc(sems["e"][e], 1)

        if self.ops["pe"]:
            @block.tensor
            def _(eng):
                run("pe", eng)
        if self.ops["act"]:
            @block.scalar
            def _(eng):
                run("act", eng)
        if self.ops["dve"]:
            @block.vector
            def _(eng):
                run("dve", eng)
        if self.ops["pool"]:
            @block.gpsimd
            def _(eng):
                run("pool", eng)
        if self.ops["sp"]:
            @block.sync
            def _(eng):
                run("sp", eng)

import contextlib
from concourse.bass_utils import run_bass_kernel_spmd

D = 2048
NKB = 32
NOWN = 17
NOTH = 15
OWN0 = NOTH * 128
WOWN = NOWN * 128
DFF = 5632
EPS = 1e-6
ARENA_BYTES = 206 * 1024
ST_BLOCKS = [(0, 9), (9, 17)]
SCALE_MLA = 192 ** -0.5
SCALE_SWA = 64 ** -0.5


class Bump:
    def __init__(self, arena, base, limit):
        self.a, self.o, self.base, self.limit = arena, base, base, limit

    def reset(self):
        self.o = self.base

    def get(self, shape, dt, p0=0, np_=128):
        n = int(np.prod(shape)) * ISZ[dt]
        n4 = (n + 3) // 4 * 4
        t = self.a.view(self.o, shape, dt, p0, np_)
        self.o += n4
        assert self.o <= self.limit, ("SBUF phase overflow", self.o, self.limit)
        return t


def build(dbg=False, stop=99):
    nc = bass.Bass("TRN2", target_bir_lowering=False)
    ext = lambda n, s, dt=F32: nc.dram_tensor(n, list(s), dt, kind="ExternalInput").ap()
    xk = ext("xk", [4096, D])
    c_in = ext("c", [1, D])
    w_ada = ext("w_ada", [D, 6 * D])
    b_ada = ext("b_ada", [1, 6 * D])
    grow = ext("grow", [4, D])
    w_in = ext("w_in", [D, 8000])
    w_uq = ext("w_uq", [768, 3072])
    w_ukv = ext("w_ukv", [512, 4096])
    w_o = ext("w_o", [D, D])
    w_up = ext("w_up", [D, 2 * DFF])
    w_down = ext("w_down", [DFF, D])
    cst = ext("cst", [128, 512])
    vecs = ext("vecs", [128, 10 + 88 * 4])
    cs_in = ext("cs", [2, 64, 4096])
    swab = ext("swab", [2, 128, 32 * 128])
    esink_in = ext("sinkrow", [1, 32 * 128])
    out = nc.dram_tensor("out", [2048, D], F32, kind="ExternalOutput").ap()
    okind = "ExternalOutput" if dbg else "Internal"
    itn = lambda n, s, dt: nc.dram_tensor(n, list(s), dt, kind=okind).ap()
    K_d = itn("K_d", [16, 128, 4096], BF16)
    V_d = itn("V_d", [32, 128, 2048], BF16)
    hT_d = itn("hT_d", [16, 128, WOWN], BF16)
    qn_d = itn("qn_d", [16, 128, WOWN], BF16)
    qr_d = itn("qr_d", [16, 64, WOWN], BF16)
    ga_d = itn("ga_d", [16, 128, WOWN], BF16)
    ks_d = itn("ks_d", [4, 128, 18 * 128], BF16)
    vs_d = itn("vs_d", [18, 128, 512], BF16)
    x1_d = itn("x1_d", [2048, D], F32)
    kr_d = itn("kr_d", [64, 4096], BF16)
    gt_d = itn("gt_d", [16, 128, WOWN], BF16)

    es = contextlib.ExitStack()
    es.enter_context(nc.allow_non_contiguous_dma(reason="layout"))
    ar = es.enter_context(nc.sbuf_tensor("arena", [128, ARENA_BYTES // 4], F32))
    pst = es.enter_context(nc.psum_tensor("ps", [128, 8 * 512], F32))
    A = Arena(ar, "sb", ARENA_BYTES)
    PSA = Arena(pst, "ps", 16 * 1024)
    P = Prog(nc)
    bank = [PSA.view(i * 2048, [512], F32) for i in range(8)]
    bankb = [PSA.view(i * 2048, [512], BF16) for i in range(8)]
    big = [PSA.view(i * 8192, [2048], F32) for i in range(2)]

    def mm(o, l, r, start=True, stop=True):
        P.op("pe", lambda e: e.matmul(o.ap, l.ap, r.ap, start=start, stop=stop), [l, r], [o])

    def tr(o, i, ident):
        P.op("pe", lambda e: e.transpose(o.ap, i.ap, ident.ap), [i, ident], [o])

    def act(o, i, func, scale=None, bias=None, accum=None, eng="act"):
        rd = [i]
        kw = {}
        if scale is not None:
            if isinstance(scale, V):
                rd.append(scale); kw["scale"] = scale.ap
            else:
                kw["scale"] = float(scale)
        if bias is not None:
            rd.append(bias); kw["bias"] = bias.ap
        wr = [o]
        if accum is not None:
            wr.append(accum); kw["accum_out"] = accum.ap
        P.op("act", lambda e: e.activation(out=o.ap, in_=i.ap, func=func, **kw), rd, wr)

    def tt(o, a, b, op, eng="dve"):
        P.op(eng, lambda e: e.tensor_tensor(out=o.ap, in0=a.ap, in1=b.ap, op=op), [a, b], [o])

    def ts(o, a, s1, s2, op0, op1=None, eng="dve"):
        rd = [a]
        v1 = s1.ap if isinstance(s1, V) else s1
        v2 = s2.ap if isinstance(s2, V) else s2
        if isinstance(s1, V): rd.append(s1)
        if isinstance(s2, V): rd.append(s2)
        if op1 is None:
            P.op(eng, lambda e: e.tensor_scalar(out=o.ap, in0=a.ap, scalar1=v1, scalar2=None, op0=op0), rd, [o])
        else:
            P.op(eng, lambda e: e.tensor_scalar(out=o.ap, in0=a.ap, scalar1=v1, scalar2=v2, op0=op0, op1=op1), rd, [o])

    def stt(o, a, s, b, op0, op1):
        rd = [a, b]
        sv = s.ap if isinstance(s, V) else s
        if isinstance(s, V): rd.append(s)
        P.op("dve", lambda e: e.scalar_tensor_tensor(out=o.ap, in0=a.ap, scalar=sv, in1=b.ap, op0=op0, op1=op1), rd, [o])

    def cp(o, i, eng="dve"):
        if eng == "act":
            act(o, i, AF.Copy)
        else:
            P.op(eng, lambda e: e.tensor_copy(out=o.ap, in_=i.ap), [i], [o])

    def recip(o, i):
        P.op("dve", lambda e: e.reciprocal(out=o.ap, in_=i.ap), [i], [o])

    def dma(o, i, q="sp"):
        return P.dma(q, o, i)

    def dreg(name, ap, lo=0, hi=1 << 40):
        return DV(ap, name, lo, hi)

    _rr = {"i": 0}

    def evac_eng():
        _rr["i"] += 1
        return "act" if _rr["i"] % 2 else "dve"

    pb = Bump(A, 0, ARENA_BYTES)
    ident_f = pb.get([128], F32)
    ident_b = pb.get([128], BF16)
    ones_b = pb.get([128], BF16)
    tri_b = pb.get([128], BF16)
    Rm_b = pb.get([64], BF16, 0, 64)
    cstf = pb.get([512], F32)
    epsc = cstf[448:449]
    bias_oth = cstf[449:450]
    flag = cstf[450:451]
    tinyc = cstf[451:452]
    vec = pb.get([10 + 88 * 4], F32)
    gq = lambda c: vec[c:c + 1]
    gkv = lambda c: vec[6 + c:7 + c]
    cw = lambda c, j: vec[10 + c * 3 + j:11 + c * 3 + j]
    cb = lambda c: vec[10 + 264 + c:11 + 264 + c]
    modv = pb.get([64], F32)
    r1_b = pb.get([2048], F32)
    r2_b = pb.get([2048], F32)
    krope_full = pb.get([4096], BF16)
    krope = krope_full.parts(0, 64)
    carry = pb.get([88, 2], F32)
    one_row = pb.get([128], F32, 0, 1)
    onepad = pb.get([128], BF16)
    one_rowb = onepad.parts(0, 1)
    esink_full = pb.get([32 * 128], BF16)
    esink_row = esink_full.parts(0, 1)
    PBASE = pb.o
    ph = Bump(A, PBASE, ARENA_BYTES)

    P.op("pool", lambda e: e.memset(krope_full.parts(64, 128).all().ap, 0.0), [], [krope_full.parts(64, 128).all()])
    cstv = dreg("cst", cst)
    dma(cstf.all(), cstv)
    dma(ident_f.all(), dreg("cst", cst[:, 0:128]))
    dma(ident_b.all(), dreg("cst", cst[:, 0:128]), "pool")
    dma(ones_b.all(), dreg("cst", cst[:, 128:256]), "pool")
    dma(tri_b.all(), dreg("cst", cst[:, 256:384]), "pool")
    dma(Rm_b.all(), dreg("cst", cst[0:64, 384:448]), "pool")
    dma(vec.all(), dreg("vecs", vecs))
    dma(one_row.all(), dreg("cst", cst[0:1, 128:256]))
    P.op("pool", lambda e: e.memset(onepad.all().ap, 0.0), [], [onepad.all()])
    P.op("pool", lambda e: e.memset(esink_full.all().ap, 0.0), [], [esink_full.all()])
    dma(one_rowb.all(), dreg("cst", cst[0:1, 128:256]), "pool")
    sk_f = ph.get([32 * 128], F32, 0, 1)
    dma(sk_f.all(), dreg("sinkrow", esink_in))
    act(esink_row.all(), sk_f.all(), AF.Exp)

    ph.reset()
    cT = ph.get([16], F32)
    cTb = ph.get([16], BF16)
    mod_row = ph.get([6 * D], F32, 0, 1)
    g_rows = ph.get([4, D], F32, 0, 1)
    pan = [ph.get([16, 1024], BF16) for _ in range(2)]
    dma(cT.all(), dreg("c", c_in.rearrange("o (kc p) -> p (o kc)", p=128)))
    dma(mod_row.all(), dreg("b_ada", b_ada))
    dma(g_rows.all(), dreg("grow", grow.rearrange("(o g) d -> o g d", o=1)))
    act(cTb.all(), cT.all(), AF.Silu)
    wav = w_ada.rearrange("(kc p) n -> p kc n", p=128)
    for hv in range(12):
        pn = pan[hv % 2]
        dma(pn.all(), dreg("w_ada", wav[:, :, hv * 1024:(hv + 1) * 1024]), "pool")
        for nb in range(2):
            bk = bank[(hv * 2 + nb) % 4].parts(0, 1)
            for kc in range(16):
                mm(bk.all(), cTb[kc:kc + 1], pn[kc, nb * 512:(nb + 1) * 512], kc == 0, kc == 15)
            c0 = hv * 1024 + nb * 512
            tt(mod_row[c0:c0 + 512], bk.all(), mod_row[c0:c0 + 512], ALU.add)
    mrow = lambda v: mod_row[v * D:(v + 1) * D]
    stt(mrow(1), mrow(1), 1.0, g_rows[0], ALU.add, ALU.mult)
    stt(mrow(4), mrow(4), 1.0, g_rows[2], ALU.add, ALU.mult)
    tt(mrow(2), mrow(2), g_rows[1], ALU.mult)
    tt(mrow(5), mrow(5), g_rows[3], ALU.mult)
    srcs = [lambda f: mod_row[1 * D + f * 128:1 * D + (f + 1) * 128], lambda f: mod_row[0 * D + f * 128:0 * D + (f + 1) * 128],
            lambda f: mod_row[4 * D + f * 128:4 * D + (f + 1) * 128], lambda f: mod_row[3 * D + f * 128:3 * D + (f + 1) * 128]]
    for vi in range(4):
        for f in range(16):
            mm(bank[4][vi * 16 + f:vi * 16 + f + 1], srcs[vi](f), one_row[0:1])
    cp(modv.all(), bank[4][0:64], "act")
    a1 = lambda f: modv[f:f + 1]
    sh1 = lambda f: modv[16 + f:17 + f]
    a2 = lambda f: modv[32 + f:33 + f]
    sh2 = lambda f: modv[48 + f:49 + f]
    for ri, dst in ((2, r1_b), (5, r2_b)):
        for nb in range(4):
            bk = bank[5 + (nb % 2)]
            mm(bk.all(), one_row.all(), mod_row[ri * D + nb * 512:ri * D + (nb + 1) * 512])
            cp(dst[nb * 512:(nb + 1) * 512], bk.all(), evac_eng())

    if dbg:
        dbg_mod = nc.dram_tensor("dbg_mod", [128, 64], F32, kind="ExternalOutput").ap()
        dbg_r = nc.dram_tensor("dbg_r", [128, 4096], F32, kind="ExternalOutput").ap()
        dma(dreg("dbg_mod", dbg_mod), modv.all())
        dma(dreg("dbg_r", dbg_r[:, 0:2048]), r1_b.all())
        dma(dreg("dbg_r", dbg_r[:, 2048:4096]), r2_b.all())
    if stop == 0:
        return finish(nc, es, P, locals())

    def rope(src_bank, w, cosv, sinv, dst, tmp):
        kb_, t1, t2, rb = tmp
        tt(t1, src_bank, cosv, ALU.mult)
        cp(kb_, src_bank, "act")
        mm(rb, Rm_b.all(), kb_)
        tt(t2, rb, sinv, ALU.mult)
        tt(dst, t1, t2, ALU.add, eng="pool")

    def lat_norm(nch, src_sb, sq, gfun, dst_fun, w, ssbank, rstd, inv_n):
        for c in range(nch):
            mm(ssbank, ones_b.all(), sq[c, 0:w], c == 0, c == nch - 1)
        act(rstd, ssbank, AF.Sqrt, scale=inv_n, bias=epsc)
        recip(rstd, rstd)
        if stop == 1.17:
            return
        for c in range(nch):
            stt(dst_fun(c), src_sb[c, 0:w], gfun(c), rstd, ALU.mult, ALU.mult)

    ph.reset()
    wkv = ph.get([16, 1088], BF16)
    wukv = ph.get([4, 4096], BF16)
    xt = [ph.get([2048], F32) for _ in range(2)]
    xs = ph.get([4, 2048], BF16)
    hT = ph.get([16, 512], BF16)
    ckv_sb = ph.get([4, 512], F32)
    sq = ph.get([4, 512], BF16)
    rstd_b = ph.get([512], F32)
    ckvn = ph.get([4, 512], BF16)
    kr_b = ph.get([512], BF16, 0, 64)
    rt1 = ph.get([512], F32, 0, 64)
    rt2 = ph.get([512], F32, 0, 64)
    cosb = ph.get([512], F32, 0, 64)
    sinb = ph.get([512], F32, 0, 64)
    kst = [ph.get([4, 512], BF16) for _ in range(2)]
    vst = [ph.get([2048], BF16) for _ in range(2)]
    ks_st = ph.get([2, 512], BF16)
    vs_st = [ph.get([4, 2, 64], BF16) for _ in range(2)]
    ssq = ph.get([8], F32)
    winv = w_in.rearrange("(kc p) n -> p kc n", p=128)
    dma(wkv[:, 0:576], dreg("w_in", winv[:, :, 768:1344]), "pool")
    dma(wkv[:, 576:1088], dreg("w_in", winv[:, :, 3392:3904]), "pool")
    wukvv = w_ukv.rearrange("(kc p) n -> p kc n", p=128)
    for hh in range(2):
        dma(wukv[:, hh * 2048:(hh + 1) * 2048], dreg("w_ukv", wukvv[:, :, hh * 2048:(hh + 1) * 2048]), "pool")
    wukv4 = A.view(wukv.boff, [4, 16, 256], BF16)
    pbk = {"i": 0}

    def nbank(lo=2, n=5):
        pbk["i"] += 1
        return lo + pbk["i"] % n

    def norm_transpose(xsrc_row0, nblk, dstT, w, afun, shfun, xs_, xt_, ssq_, x_dram, x_name):
        for blk in range(nblk):
            xb = xt_[blk % 2]
            dma(xb.all(), dreg(x_name, x_dram[xsrc_row0 + blk * 128: xsrc_row0 + (blk + 1) * 128, :]))
            act(xs_[blk], xb.all(), AF.Square, accum=ssq_[blk:blk + 1])
            act(ssq_[4 + blk:5 + blk], ssq_[blk:blk + 1], AF.Sqrt, scale=1.0 / D, bias=epsc)
            recip(ssq_[4 + blk:5 + blk], ssq_[4 + blk:5 + blk])
            ts(xs_[blk], xb.all(), ssq_[4 + blk:5 + blk], None, ALU.mult)
        for f in range(16):
            bk = bankb[f % 2]
            for blk in range(nblk):
                tr(bk[blk * 128:(blk + 1) * 128], xs_[blk, f * 128:(f + 1) * 128], ident_b.all())
            if f % 2 == 0:
                act(dstT(f), bk[0:w], AF.Identity, scale=afun(f), bias=shfun(f))
            else:
                ts(dstT(f), bk[0:w], afun(f), shfun(f), ALU.mult, ALU.add)

    for it in range(0 if stop > 1.9 else 4, 8 if stop > 1.9 else 5):
        r0 = it * 512
        dma(cosb.all(), dreg("cs", cs_in[0, :, r0:r0 + 512]))
        dma(sinb.all(), dreg("cs", cs_in[1, :, r0:r0 + 512]))
        norm_transpose(r0, 4, lambda f: hT[f], 512, a1, sh1, xs, xt, ssq, xk, "xk")
        if r0 + 512 > OWN0:
            c0 = max(0, OWN0 - r0)
            o0 = r0 + c0 - OWN0
            dma(dreg("hT_d", hT_d.rearrange("f p t -> p f t")[:, :, o0:o0 + 512 - c0]), hT[:, c0:512])
        if stop == 1.1:
            break
        for c in range(4):
            bk = bank[nbank()]
            for kc in range(16):
                mm(bk.all(), wkv[kc, c * 128:(c + 1) * 128], hT[kc], kc == 0, kc == 15)
            cp(ckv_sb[c], bk.all(), "dve")
            act(sq[c], bk.all(), AF.Square)
        if stop == 1.15:
            break
        lat_norm(4, ckv_sb, sq, gkv, lambda c: ckvn[c], 512, bank[7].all(), rstd_b.all(), 1.0 / 512)
        if stop == 1.2:
            break
        bk = bank[nbank()].parts(0, 64)
        for kc in range(16):
            mm(bk.all(), wkv[kc, 512:576], hT[kc], kc == 0, kc == 15)
        rope(bk.all(), 512, cosb.all(), sinb.all(), krope[r0:r0 + 512],
             (kr_b.all(), rt1.all(), rt2.all(), bank[nbank()].parts(0, 64).all()))
        if stop == 1.3:
            break
        if r0 + 512 > 14 * 128:
            c0 = max(0, 14 * 128 - r0)
            s0 = r0 + c0 - 14 * 128
            for c in range(2):
                bk = bank[nbank()]
                for kc in range(16):
                    mm(bk.all(), wkv[kc, 576 + c * 128:576 + (c + 1) * 128], hT[kc], kc == 0, kc == 15)
                cp(ks_st[c], bk.all(), evac_eng())
            for kvh in range(4):
                src = ks_st.parts((kvh % 2) * 64, (kvh % 2) * 64 + 64)[kvh // 2, c0:512]
                for dup in range(2):
                    dma(dreg("ks_d", ks_d[kvh, dup * 64:(dup + 1) * 64, s0:s0 + 512 - c0]), src)
            for blk in range(c0 // 128, 4):
                bk = bank[nbank()]
                for kc in range(16):
                    mm(bk[0:256], hT[kc, blk * 128:(blk + 1) * 128], wkv[kc, 832:1088], kc == 0, kc == 15)
                vv = vs_st[blk % 2]
                bk4 = PSA.view(bk.boff, [4, 64], F32)
                for dup in range(2):
                    cp(vv[:, dup, :], bk4.all(), evac_eng())
                sb = (r0 + blk * 128) // 128 - 14
                dma(dreg("vs_d", vs_d[sb]), vv.all())
        if stop == 1.4:
            break
        for h in range(16):
            bk = bank[nbank()]
            for c in range(4):
                mm(bk.all(), wukv[c, h * 256:h * 256 + 128], ckvn[c], c == 0, c == 3)
            ks_ = kst[(h // 4) % 2]
            cp(ks_[h % 4], bk.all(), evac_eng())
            if h % 4 == 3:
                dma(dreg("K_d", K_d[h - 3:h + 1].rearrange("h p t -> p h t")[:, :, r0:r0 + 512]), ks_.all())
        if stop == 1.5:
            break
        for blk in range(4):
            vs_ = vst[blk % 2]
            for hq in range(4):
                bk = bank[nbank()]
                for c in range(4):
                    mm(bk.all(), ckvn[c, blk * 128:(blk + 1) * 128], wukv4[c, hq * 4:(hq + 1) * 4, 128:256], c == 0, c == 3)
                cp(vs_[hq * 512:(hq + 1) * 512], bk.all(), evac_eng())
            dma(dreg("V_d", V_d[it * 4 + blk]), vs_.all())
    if dbg:
        dma(dreg("kr_d", kr_d), krope.all())
    if stop < 2:
        return finish(nc, es, P, locals())

    def mmg(o, l, r, start, stop_):
        P.op("pe", lambda e: e.matmul(o.ap, l.ap, r.ap, start=start, stop=stop_, skip_group_check=True), [l, r], [o])

    es3 = A.view(esink_full.boff, [16, 2, 128], BF16)
    wuqv = w_uq.rearrange("(kc p) n -> p kc n", p=128)
    wov = w_o.rearrange("(kc p) n -> p kc n", p=128)
    wupv = w_up.rearrange("(kc p) n -> p kc n", p=128)
    wdnv = w_down.rearrange("(kc p) n -> p kc n", p=128)
    hTdv = hT_d.rearrange("f p t -> p f t")
    gadv = ga_d.rearrange("f p t -> p f t")

    for st, (b0, b1) in enumerate(ST_BLOCKS):
        Wst = (b1 - b0) * 128
        o0 = b0 * 128
        tw = 384 if Wst % 384 == 0 else 512
        tiles = [(t, tw) for t in range(0, Wst, tw)]
        ph.reset()
        h2off = ARENA_BYTES - 16 * Wst * 2
        h2T = A.view(h2off, [16, Wst], BF16)
        mark_ffn = ph.o
        gated = ph.get([16, Wst], BF16)
        mark_g = ph.o
        hTs = ph.get([16, Wst], BF16)
        panA = [ph.get([16, 512], BF16) for _ in range(2)]
        mark_p2 = ph.o
        cosq = ph.get([Wst], F32, 0, 64)
        sinq = ph.get([Wst], F32, 0, 64)
        cqn = ph.get([6, Wst], BF16)
        mark_q = ph.o
        cq_sb = ph.get([6, 512], F32)
        sq6 = ph.get([6, 512], BF16)
        rstd6 = ph.get([512], F32)
        ph.o = mark_q
        uqp = [ph.get([6, 192], BF16) for _ in range(2)]
        qn_st = [ph.get([Wst], BF16) for _ in range(2)]
        qr_st = [ph.get([Wst], BF16, 0, 64) for _ in range(2)]
        qkb = [ph.get([512], BF16, 0, 64) for _ in range(2)]
        qt1 = [ph.get([512], F32, 0, 64) for _ in range(2)]
        qt2 = [ph.get([512], F32, 0, 64) for _ in range(2)]
        dma(hTs.all(), dreg("hT_d", hTdv[:, :, o0:o0 + Wst]))
        dma(cosq.all(), dreg("cs", cs_in[0, :, OWN0 + o0:OWN0 + o0 + Wst]))
        dma(sinq.all(), dreg("cs", cs_in[1, :, OWN0 + o0:OWN0 + o0 + Wst]))
        dma(panA[0].all(), dreg("w_in", winv[:, :, 0:512]), "pool")
        dma(panA[1][:, 0:256], dreg("w_in", winv[:, :, 512:768]), "pool")
        for (t0, w) in tiles:
            for c in range(6):
                bk = bank[nbank()]
                for kc in range(16):
                    mm(bk[0:w], panA[c // 4][kc, (c % 4) * 128:(c % 4 + 1) * 128], hTs[kc, t0:t0 + w], kc == 0, kc == 15)
                cp(cq_sb[c, 0:w], bk[0:w], "dve")
                act(sq6[c, 0:w], bk[0:w], AF.Square)
            lat_norm(6, cq_sb, sq6, gq, lambda c: cqn[c, t0:t0 + w], w, bank[7][0:w], rstd6[0:w], 1.0 / 768)
        qitems = [(h, ti, t0, w) for h in range(16) for ti, (t0, w) in enumerate(tiles)]

        def q_s1(i):
            h, ti, t0, w = qitems[i]
            up = uqp[h % 2]
            if ti == 0:
                dma(up.all(), dreg("w_uq", wuqv[:, :, h * 192:(h + 1) * 192]), "pool")
            bk = bank[i % 2]
            for kc in range(6):
                mm(bk[0:w], up[kc, 0:128], cqn[kc, t0:t0 + w], kc == 0, kc == 5)
            bk2 = bank[2 + i % 2].parts(0, 64)
            for kc in range(6):
                mm(bk2[0:w], up[kc, 128:192], cqn[kc, t0:t0 + w], kc == 0, kc == 5)

        def q_s2(i):
            h, ti, t0, w = qitems[i]
            cp(qn_st[h % 2][t0:t0 + w], bank[i % 2][0:w], evac_eng())
            bk2 = bank[2 + i % 2].parts(0, 64)
            rope(bk2[0:w], w, cosq[t0:t0 + w], sinq[t0:t0 + w], qr_st[h % 2][t0:t0 + w],
                 (qkb[i % 2][0:w], qt1[i % 2][0:w], qt2[i % 2][0:w], bank[4 + i % 2].parts(0, 64)[0:w]))
            if ti == len(tiles) - 1:
                dma(dreg("qn_d", qn_d[h, :, o0:o0 + Wst]), qn_st[h % 2].all())
                dma(dreg("qr_d", qr_d[h, :, o0:o0 + Wst]), qr_st[h % 2].all())

        q_s1(0)
        for i in range(len(qitems)):
            if i + 1 < len(qitems):
                q_s1(i + 1)
            q_s2(i)
        if stop == 2.1:
            break
        ph.o = mark_p2
        nsb = (b1 - b0) + 1
        EBh = [ph.get([2, 4, 2, 128], BF16) for _ in range(2)]
        ebst = [ph.get([8 * 128], F32) for _ in range(1)]
        ks_sw = ph.get([4, nsb * 128], BF16)
        vs_sw = ph.get([nsb, 512], BF16)
        qsz = [ph.get([4, Wst], BF16) for _ in range(2)]
        ptraw = [ph.get([4, 128], F32) for _ in range(2)] * 2
        ptb = [ph.get([4, 128], BF16) for _ in range(4)]
        rden = [ph.get([512], F32) for _ in range(2)]
        _z0 = qsz[0].parts(64, 128).all()
        _z1 = qsz[1].parts(0, 64).all()
        P.op("pool", lambda e, _z0=_z0: e.memset(_z0.ap, 0.0), [], [_z0])
        P.op("pool", lambda e, _z1=_z1: e.memset(_z1.ap, 0.0), [], [_z1])
        for kvh in range(4):
            dma(ks_sw[kvh], dreg("ks_d", ks_d[kvh, :, b0 * 128:(b0 + nsb) * 128]))
        dma(vs_sw.all(), dreg("vs_d", vs_d[b0:b0 + nsb].rearrange("s p c -> p s c")))
        for hk in range(4):
            pn = panA[hk % 2]
            dma(pn.all(), dreg("w_in", winv[:, :, 1344 + hk * 512:1344 + (hk + 1) * 512]), "pool")
            EBc = EBh[hk % 2]
            EBcf = A.view(EBc.boff, [2, 8 * 128], BF16)
            for pi in range(2):
                dma(ebst[0].all(), dreg("swab", swab[pi, :, hk * 1024:(hk + 1) * 1024]))
                act(EBcf[pi], ebst[0].all(), AF.Exp)
            for (t0, w) in tiles:
                for c in range(4):
                    bk = bank[nbank()]
                    for kc in range(16):
                        mm(bk[0:w], pn[kc, c * 128:(c + 1) * 128], hTs[kc, t0:t0 + w], kc == 0, kc == 15)
                    cp(qsz[0].parts(0, 64)[c, t0:t0 + w], bk.parts(0, 64)[0:w], "act")
                    cp(qsz[1].parts(64, 128)[c, t0:t0 + w], bk.parts(64, 128)[0:w], "dve")
            chains = [(b, e) for b in range(b0, b1) for e in range(2)]

            def swa_S(i):
                b, e = chains[i]
                lb = b - b0
                for pi in range(2):
                    sbl = lb + pi
                    S = PSA.view(bank[2 * (i % 2) + pi].boff, [4, 128], F32)
                    mm(S.all(), ks_sw[hk, sbl * 128:(sbl + 1) * 128], qsz[e][:, lb * 128:(lb + 1) * 128])

            swa_S(0)
            for i, (b, e) in enumerate(chains):
                lb = b - b0
                if i + 1 < len(chains):
                    swa_S(i + 1)
                for pi in range(2):
                    S = PSA.view(bank[2 * (i % 2) + pi].boff, [4, 128], F32)
                    gsb = b + pi
                    pr = ptraw[(i % 2) * 2 + pi]
                    act(pr.all(), S.all(), AF.Exp, scale=SCALE_SWA, bias=(bias_oth if gsb <= 1 else None))
                    tt(ptb[(i % 2) * 2 + pi].all(), pr.all(), EBc[pi, :, e, :], ALU.mult)
                ob = PSA.view(bank[4 + 2 * (i % 2)].boff, [4, 128], F32)
                db = PSA.view(bank[5 + 2 * (i % 2)].boff, [4, 128], F32)
                for pi in range(2):
                    sbl = lb + pi
                    mmg(ob.all(), vs_sw[sbl, hk * 128:(hk + 1) * 128], ptb[(i % 2) * 2 + pi].all(), pi == 0, pi == 1)
                for pi in range(2):
                    mmg(db.all(), ones_b.all(), ptb[(i % 2) * 2 + pi].all(), pi == 0, False)
                mmg(db.all(), onepad.all(), es3[4 * hk:4 * hk + 4, e, :], False, True)
                rd4 = A.view(rden[i % 2].boff, [4, 128], F32)
                act(rd4.all(), db.all(), AF.Ln)
                act(rd4.all(), rd4.all(), AF.Exp, scale=-1.0)
                tt(gated.parts(e * 64, e * 64 + 64)[4 * hk:4 * hk + 4, lb * 128:(lb + 1) * 128],
                   ob.parts(e * 64, e * 64 + 64).all(), rd4.parts(e * 64, e * 64 + 64).all(), ALU.mult)
        if stop == 2.2:
            break
        ph.o = mark_p2
        sig = ph.get([512], F32)
        ga_st = [ph.get([4, Wst], BF16) for _ in range(2)]
        for gi in (4, 5, 6, 7, 0, 1, 2, 3):
            pn = panA[gi % 2]
            dma(pn.all(), dreg("w_in", winv[:, :, 3904 + gi * 512:3904 + (gi + 1) * 512]), "pool")
            for (t0, w) in tiles:
                for c in range(4):
                    ch = (gi % 4) * 4 + c
                    bk = bank[nbank()]
                    for kc in range(16):
                        mm(bk[0:w], pn[kc, c * 128:(c + 1) * 128], hTs[kc, t0:t0 + w], kc == 0, kc == 15)
                    if gi >= 4:
                        act(sig[0:w], bk[0:w], AF.Sigmoid)
                        tt(gated[ch, t0:t0 + w], sig[0:w], gated[ch, t0:t0 + w], ALU.mult)
                    else:
                        act(ga_st[gi % 2][c, t0:t0 + w], bk[0:w], AF.Sigmoid)
            if gi < 4:
                dma(dreg("ga_d", gadv[:, gi * 4:gi * 4 + 4, o0:o0 + Wst]), ga_st[gi % 2].all())
        if stop == 2.3:
            break
        ph.o = mark_g
        Kb = [ph.get([4096], BF16) for _ in range(2)]
        Vb = [ph.get([32, 256], BF16) for _ in range(2)]
        qn = [ph.get([Wst], BF16) for _ in range(2)]
        qr = [ph.get([Wst], BF16) for _ in range(2)]
        for _q in qr:
            P.op("pool", lambda e, _q=_q: e.memset(_q.parts(64, 128).all().ap, 0.0), [], [_q.parts(64, 128).all()])
        ga = [ph.get([Wst], BF16) for _ in range(2)]
        pt = [ph.get([512], BF16) for _ in range(8)]
        rdm = [ph.get([512], F32) for _ in range(2)]
        otmp = [ph.get([512], F32) for _ in range(2)]
        pacc = [[ph.get([512], F32) for _ in range(3)] for _ in range(2)]
        phi = [ph.get([512], BF16) for _ in range(2)]
        plo = [ph.get([512], BF16) for _ in range(2)]
        nkb = NOTH + b1
        items = []
        for h in range(16):
            for ti, (t0, w) in enumerate(tiles):
                qb0 = b0 + t0 // 128
                nb_ = w // 128
                kl = [(kb, 0, False) for kb in range(NOTH + qb0)] + [(NOTH + qb0 + s_, s_ * 128, True) for s_ in range(nb_)]
                for idx, (kb, c0, diag) in enumerate(kl):
                    items.append((h, ti, t0, w, kb, c0, diag, idx == 0, idx == len(kl) - 1))

        def mla_loads(h):
            dma(Kb[h % 2][0:nkb * 128], dreg("K_d", K_d[h, :, 0:nkb * 128]))
            if h % 2 == 0:
                dma(Vb[(h // 2) % 2][0:nkb], dreg("V_d", V_d[0:nkb, :, (h // 2) * 256:(h // 2 + 1) * 256].rearrange("k p c -> p k c")))
            dma(qn[h % 2].all(), dreg("qn_d", qn_d[h, :, o0:o0 + Wst]))
            dma(qr[h % 2].parts(0, 64).all(), dreg("qr_d", qr_d[h, :, o0:o0 + Wst]))
            dma(ga[h % 2].all(), dreg("ga_d", ga_d[h, :, o0:o0 + Wst]))

        def mla_S(i):
            h, ti, t0, w, kb, c0, diag, first, last = items[i]
            if i == 0 or items[i - 1][0] != h:
                mla_loads(h)
            S = bank[4 + i % 4]
            mmg(S[c0:w], Kb[h % 2][kb * 128:(kb + 1) * 128], qn[h % 2][t0 + c0:t0 + w], True, False)
            mmg(S[c0:w], krope_full[kb * 128:(kb + 1) * 128], qr[h % 2][t0 + c0:t0 + w], False, True)

        LA = 3
        nS = 0
        tcount = 0
        pending = []
        for i, (h, ti, t0, w, kb, c0, diag, first, last) in enumerate(items):
            while nS <= min(i + LA, len(items) - 1):
                mla_S(nS)
                nS += 1
            S = bank[4 + i % 4]
            p = pt[i % 8]
            act(p[c0:w], S[c0:w], AF.Exp, scale=SCALE_MLA, bias=(bias_oth if kb <= NOTH else None))
            if diag:
                tt(p[c0:c0 + 128], p[c0:c0 + 128], tri_b.all(), ALU.mult)
            oacc, dacc = bank[2 * (tcount % 2)], bank[2 * (tcount % 2) + 1]
            vb_ = Vb[(h // 2) % 2]
            mmg(oacc[c0:w], vb_[kb, (h % 2) * 128:(h % 2 + 1) * 128], p[c0:w], first, last)
            if first:
                tidx = 0
            ai = tidx % 2
            pa_ = pacc[tcount % 2][ai]
            aeng = "dve"
            if tidx < 2:
                assert c0 == 0
                cp(pa_[0:w], p[0:w], aeng)
            else:
                tt(pa_[c0:w], pa_[c0:w], p[c0:w], ALU.add, eng=aeng)
            tidx += 1
            if last:
                hi_, lo_ = phi[tcount % 2], plo[tcount % 2]
                pa_ = pacc[tcount % 2][0]
                tt(pa_[0:w], pa_[0:w], pacc[tcount % 2][1][0:w], ALU.add)
                cp(hi_[0:w], pa_[0:w], "dve")
                tt(lo_[0:w], pa_[0:w], hi_[0:w], ALU.subtract)

                def _epi(tc=tcount, w=w, h=h, t0=t0, oacc=oacc, dacc=dacc):
                    rd_ = rdm[tc % 2]
                    ot_ = otmp[tc % 2]
                    mmg(dacc[0:w], ones_b.all(), phi[tc % 2][0:w], True, False)
                    mmg(dacc[0:w], ones_b.all(), plo[tc % 2][0:w], False, True)
                    act(rd_[0:w], dacc[0:w], AF.Ln, bias=tinyc)
                    act(rd_[0:w], rd_[0:w], AF.Exp, scale=-1.0)
                    tt(ot_[0:w], oacc[0:w], rd_[0:w], ALU.mult)
                    tt(ot_[0:w], ot_[0:w], ga[h % 2][t0:t0 + w], ALU.mult, eng="pool")
                    tt(gated[h, t0:t0 + w], ot_[0:w], gated[h, t0:t0 + w], ALU.add, eng="pool")
                pending.append((i + 5, _epi))
                tcount += 1
            while pending and (pending[0][0] <= i or i == len(items) - 1):
                pending.pop(0)[1]()
        if dbg:
            dma(dreg("gt_d", gt_d.rearrange("f p t -> p f t")[:, :, o0:o0 + Wst]), gated.all())
        if stop == 2.4:
            break
        ph.o = mark_g
        ph.limit = h2off
        wo = ph.get([16, 2048], BF16)
        xb = ph.get([2048], F32)
        x1 = ph.get([2048], F32)
        xs2 = ph.get([2048], BF16)
        ss = ph.get([8], F32)
        for n in range(4):
            dma(wo[:, n * 512:(n + 1) * 512], dreg("w_o", wov[:, :, n * 512:(n + 1) * 512]), "pool")
        junkw = ph.get([2048], BF16)

        def wo_mm(b):
            lb = b - b0
            for n in range(4):
                for kc in range(16):
                    mm(big[lb % 2][n * 512:(n + 1) * 512], gated[kc, lb * 128:(lb + 1) * 128], wo[kc, n * 512:(n + 1) * 512], kc == 0, kc == 15)

        wo_mm(b0)
        for b in range(b0, b1):
            lb = b - b0
            if b + 1 < b1:
                wo_mm(b + 1)
            bg = big[lb % 2]
            dma(xb.all(), dreg("xk", xk[OWN0 + b * 128:OWN0 + (b + 1) * 128, :]))
            act(junkw.all(), bg.all(), AF.Square, accum=ss[0:1])
            act(ss[1:2], ss[0:1], AF.Sqrt, scale=1.0 / D, bias=epsc)
            recip(ss[1:2], ss[1:2])
            stt(x1.all(), bg.all(), ss[1:2], r1_b.all(), ALU.mult, ALU.mult)
            tt(x1.all(), x1.all(), xb.all(), ALU.add)
            if b >= 1:
                dma(dreg("x1_d", x1_d[(b - 1) * 128:b * 128, :], (b - 1) * 128, b * 128), x1.all())
            act(junkw.all(), x1.all(), AF.Square, accum=ss[2:3])
            act(ss[3:4], ss[2:3], AF.Sqrt, scale=1.0 / D, bias=epsc)
            recip(ss[3:4], ss[3:4])
            ts(xs2.all(), x1.all(), ss[3:4], None, ALU.mult)
            for f in range(16):
                bk = bankb[(lb % 2) * 4 + f // 4]
                tr(bk[(f % 4) * 128:(f % 4 + 1) * 128], xs2[f * 128:(f + 1) * 128], ident_b.all())
            for f in range(16):
                bk = bankb[(lb % 2) * 4 + f // 4]
                if f % 2 == 0:
                    act(h2T[f, lb * 128:(lb + 1) * 128], bk[(f % 4) * 128:(f % 4 + 1) * 128], AF.Identity, scale=a2(f), bias=sh2(f))
                else:
                    ts(h2T[f, lb * 128:(lb + 1) * 128], bk[(f % 4) * 128:(f % 4 + 1) * 128], a2(f), sh2(f), ALU.mult, ALU.add)
        if stop == 2.5:
            break
        ph.o = mark_ffn
        aT = ph.get([44, 1024], BF16)
        mark_dn = ph.o
        upan = [ph.get([16, 256], BF16) for _ in range(2)]
        U = [[ph.get([tw + 2], F32) for _ in range(2)] for _ in range(2)]
        c1 = ph.get([512], F32)
        cx = [ph.get([512], F32) for _ in range(2)]
        sg = ph.get([512], F32)
        hal = 128 if st == 0 else 0
        for c in range(44):
            up = upan[c % 2]
            dma(up[:, 0:128], dreg("w_up", wupv[:, :, c * 128:(c + 1) * 128]), "pool")
            dma(up[:, 128:256], dreg("w_up", wupv[:, :, (44 + c) * 128:(45 + c) * 128]), "pool")
            for ti, (t0, w) in enumerate(tiles):
                for br in range(2):
                    cc = c + 44 * br
                    bk = bank[2 * (ti % 2) + br]
                    for kc in range(16):
                        mm(bk[0:w], up[kc, br * 128:(br + 1) * 128], h2T[kc, t0:t0 + w], kc == 0, kc == 15)
                    Ub = U[ti % 2][br]
                    cp(Ub[2:2 + w], bk[0:w], "act")
                    if ti == 0:
                        if st == 0:
                            P.op("dve", lambda e, Ub=Ub: e.memset(Ub[0:2].ap, 0.0), [], [Ub[0:2]])
                        else:
                            cp(Ub[0:2], carry[cc], "dve")
                    else:
                        cp(Ub[0:2], U[(ti - 1) % 2][br][w:w + 2], "dve")
                    if st == 0 and ti == 0:
                        ts(Ub[128:130], Ub[128:130], flag, None, ALU.mult)
                    ts(c1[0:w], Ub[0:w], cw(cc, 0), cb(cc), ALU.mult, ALU.add)
                    stt(c1[0:w], Ub[1:w + 1], cw(cc, 1), c1[0:w], ALU.mult, ALU.add)
                    stt(cx[br][0:w], Ub[2:w + 2], cw(cc, 2), c1[0:w], ALU.mult, ALU.add)
                    if ti == len(tiles) - 1:
                        cp(carry[cc], Ub[w:w + 2], "dve")
                act(sg[0:w], cx[0][0:w], AF.Silu)
                off = hal if ti == 0 else 0
                tt(aT[c, t0 + off - hal:t0 + w - hal], sg[off:w], cx[1][off:w], ALU.mult)
        if stop == 2.6:
            break
        ph.o = mark_dn
        ph.limit = ARENA_BYTES
        ytok = ph.get([4, 2048], F32)
        dpan = [ph.get([44, 128], BF16) for _ in range(2)]
        ysb = [ph.get([512], F32) for _ in range(2)]
        x1b = ph.get([2048], F32)
        junk = ph.get([2048], BF16)
        ss2 = ph.get([8], F32)
        for dt_ in range(2):
            a0 = dt_ * 512
            for f in range(16):
                dp = dpan[f % 2]
                dma(dp.all(), dreg("w_down", wdnv[:, :, f * 128:(f + 1) * 128]), "pool")
                bk = bank[f % 2]
                for kc in range(44):
                    mm(bk.all(), dp[kc], aT[kc, a0:a0 + 512], kc == 0, kc == 43)
                cp(ysb[f % 2].all(), bk.all(), evac_eng())
                tb = bank[2 + f % 2]
                for blk in range(4):
                    tr(tb[blk * 128:(blk + 1) * 128], ysb[f % 2][blk * 128:(blk + 1) * 128], ident_f.all())
                tb4 = PSA.view(tb.boff, [4, 128], F32)
                cp(ytok[:, f * 128:(f + 1) * 128], tb4.all(), evac_eng())
            for blk in range(4):
                row0 = st * 1024 + a0 + blk * 128
                dma(x1b.all(), dreg("x1_d", x1_d[row0:row0 + 128, :], row0, row0 + 128))
                act(junk.all(), ytok[blk], AF.Square, accum=ss2[0:1])
                act(ss2[1:2], ss2[0:1], AF.Sqrt, scale=1.0 / D, bias=epsc)
                recip(ss2[1:2], ss2[1:2])
                stt(ytok[blk], ytok[blk], ss2[1:2], r2_b.all(), ALU.mult, ALU.mult)
                tt(ytok[blk], ytok[blk], x1b.all(), ALU.add)
                dma(dreg("out", out[row0:row0 + 128, :], row0, row0 + 128), ytok[blk])
    return finish(nc, es, P, locals())


def finish(nc, es, P, L):
    out = L["out"]
    P.wait_events("sp", list(P.all_dma_events))
    sems = {"e": {}, "d": {}}
    for e in P.ENG:
        sems["e"][e] = es.enter_context(nc.semaphore("s_" + e))
    for key in P.dma_cnt:
        sems["d"][key] = es.enter_context(nc.semaphore("d_%s_%d" % key))
    with nc.Block() as block:
        P.emit(block, sems)
    es.close()
    return nc


def _t5_bucket(dist):
    max_exact = 16
    n = np.maximum(dist, 0)
    large = max_exact + (np.log(np.maximum(n, 1).astype(np.float32) / max_exact)
                         / np.float32(math.log(128 / max_exact)) * (32 - max_exact)).astype(np.int32)
    large = np.minimum(large, 31)
    return np.where(n < max_exact, n, large)


def _consts(j):
    cst = np.zeros((128, 512), np.float32)
    cst[:, 0:128] = np.eye(128, dtype=np.float32)
    cst[:, 128:256] = 1.0
    k = np.arange(128)[:, None]
    q = np.arange(128)[None, :]
    cst[:, 256:384] = (k <= q).astype(np.float32)
    Rm = np.zeros((64, 64), np.float32)
    for d in range(32):
        Rm[d + 32, d] = -1.0
        Rm[d, d + 32] = 1.0
    cst[0:64, 384:448] = Rm
    cst[:, 448] = EPS
    cst[:, 449] = -30000.0 if j == 0 else 0.0
    cst[:, 450] = 0.0 if j == 0 else 1.0
    cst[:, 451] = 1e-30
    return cst


def _rope_tab(j):
    r = np.arange(4096)
    pos = (r if j == 1 else np.maximum(r - 2048, 0)).astype(np.float32)
    inv = (np.float32(10000.0) ** (-np.arange(0, 64, 2, dtype=np.float32) / np.float32(64))).astype(np.float32)
    ang = pos[:, None] * inv[None, :]
    ang = np.concatenate([ang, ang], axis=-1)
    return np.stack([np.cos(ang).T, np.sin(ang).T]).astype(np.float32)


def prep_inputs(inp):
    f = lambda a: np.ascontiguousarray(np.asarray(a, dtype=np.float32))
    x = f(inp["x"])
    rel_bias = f(inp["rel_bias"])
    k = np.arange(128)[:, None]
    q = np.arange(128)[None, :]
    bt = np.zeros((2, 128, 32, 128), np.float32)
    d_prev = 128 + q - k
    d_cur = q - k
    bprev = rel_bias[_t5_bucket(d_prev)]
    bcur = rel_bias[_t5_bucket(d_cur)]
    bt[0] = np.where((k > q)[:, None, :], bprev.transpose(0, 2, 1), np.float32(-30000.0))
    bt[1] = np.where((k <= q)[:, None, :], bcur.transpose(0, 2, 1), np.float32(-30000.0))
    swab = bt.reshape(2, 128, 32 * 128)
    sinkrow = np.repeat(f(inp["sinks"])[0], 128)[None, :].astype(np.float32)
    vecs = np.zeros((128, 10 + 88 * 4), np.float32)
    vecs[:, 0:6] = f(inp["g_q_lat"])[0].reshape(6, 128).T
    vecs[:, 6:10] = f(inp["g_kv_lat"])[0].reshape(4, 128).T
    cwv = f(inp["conv_w"])[0]
    vecs[:, 10:10 + 264] = cwv.reshape(3, 88, 128).transpose(2, 1, 0).reshape(128, 264)
    vecs[:, 274:274 + 88] = f(inp["conv_b"])[0].reshape(88, 128).T
    grow = np.stack([f(inp["g_pre_mix"])[0], f(inp["g_post_mix"])[0], f(inp["g_pre_ffn"])[0], f(inp["g_post_ffn"])[0]])
    shared = dict(w_ada=f(inp["w_ada"])[0], b_ada=f(inp["b_ada"]), grow=grow, w_in=f(inp["w_in"])[0],
                  w_uq=f(inp["w_uq"])[0], w_ukv=f(inp["w_ukv"])[0], w_o=f(inp["w_o"])[0], w_up=f(inp["w_up"])[0],
                  w_down=f(inp["w_down"])[0], vecs=vecs, swab=swab, sinkrow=sinkrow)
    csts = [_consts(0), _consts(1)]
    ropes = [_rope_tab(0), _rope_tab(1)]
    zeros = np.zeros((2048, D), np.float32)
    maps = []
    for core in range(8):
        b, j = core // 2, core % 2
        xk = x[b] if j == 1 else np.concatenate([zeros, x[b][:2048]], axis=0)
        m = dict(shared)
        m.update(xk=np.ascontiguousarray(xk), c=f(inp["c"])[b:b + 1], cst=csts[j], cs=ropes[j])
        maps.append(m)
    return maps


_NC_CACHE = {}


def kernel(**inputs):
    if "nc" not in _NC_CACHE:
        _NC_CACHE["nc"] = build(False)
    nc = _NC_CACHE["nc"]
    maps = prep_inputs(inputs)
    res = run_bass_kernel_spmd(nc, maps, core_ids=list(range(8)))
    outp = np.zeros((4, 4096, D), np.float32)
    for core in range(8):
        b, j = core // 2, core % 2
        outp[b, j * 2048:(j + 1) * 2048] = res.results[core]["out"]
    return outp
```
